# Optimizing a Trainium2 kernel written in Bass

```python
import jax, jax.numpy as jnp
from jax import lax
import numpy as np

D_MODEL = 4096
BATCH = 8
SEQ = 2048
DEPTH = 1

CHUNK = 64

MIX_WIDTH = D_MODEL
SB_WIDTH = MIX_WIDTH // 2
RWKV_WIDTH = MIX_WIDTH - SB_WIDTH
SB_HEAD_DIM = 128
SB_HEADS = SB_WIDTH // SB_HEAD_DIM
RWKV_HEAD_SIZE = 64
RWKV_HEADS = RWKV_WIDTH // RWKV_HEAD_SIZE
DECAY_LORA = 96
AAA_LORA = 96
GATE_LORA = 256
RWKV_COLS = 3 * RWKV_WIDTH + DECAY_LORA + AAA_LORA + GATE_LORA
IN_COLS = 3 * SB_WIDTH + RWKV_COLS
Q_BLOCK = 128

D_FF = 11008
CONV_WIDTH = 3

NORM_EPS = 1e-6
GN_EPS = 64e-5

kernel_name = "hybrid_stickbreak_rwkv7_convffn_adaln"


def rmsnorm(x, g):
    xf = x.astype(jnp.float32)
    y = xf * lax.rsqrt(jnp.mean(xf * xf, axis=-1, keepdims=True) + NORM_EPS)
    return (y * g.astype(jnp.float32)).astype(x.dtype)


def modulate(h, shift, scale):
    return h * (1 + scale[:, None, :]) + shift[:, None, :]


def stick_breaking_attention(q, k, v):
    seq = q.shape[2]
    scale = np.float32(1.0 / np.sqrt(q.shape[-1]))
    outs = []
    for i in range(seq // Q_BLOCK):
        start, end = i * Q_BLOCK, (i + 1) * Q_BLOCK
        qb = q[:, :, start:end]
        kb = k[:, :, :end]
        vb = v[:, :, :end]
        z = jnp.einsum('bhqd,bhkd->bhqk', qb, kb).astype(jnp.float32) * scale
        t_idx = start + jnp.arange(Q_BLOCK)[:, None]
        s_idx = jnp.arange(end)[None, :]
        causal = s_idx < t_idx
        log_1m = jnp.where(causal, -jax.nn.softplus(z), 0.0)
        later = lax.cumsum(log_1m, axis=3, reverse=True) - log_1m
        a_w = jnp.where(causal, jnp.exp(jax.nn.log_sigmoid(z) + later), 0.0)
        outs.append(jnp.einsum('bhqk,bhkd->bhqd', a_w.astype(v.dtype), vb))
    return jnp.concatenate(outs, axis=2)


def rwkv7_recurrence(r, w, k, v, kk, a):
    bsz, _, nh, n = r.shape

    def step(state, inp):
        r_t, w_t, k_t, v_t, kk_t, a_t = inp
        sa = jnp.einsum('bhvk,bhk->bhv', state, -kk_t)
        state = (state * w_t[:, :, None, :]
                 + sa[..., None] * (kk_t * a_t)[:, :, None, :]
                 + v_t[..., None] * k_t[:, :, None, :])
        y = jnp.einsum('bhvk,bhk->bhv', state, r_t)
        return state, y

    xs = (jnp.moveaxis(r, 1, 0), jnp.moveaxis(w, 1, 0), jnp.moveaxis(k, 1, 0),
          jnp.moveaxis(v, 1, 0), jnp.moveaxis(kk, 1, 0), jnp.moveaxis(a, 1, 0))
    s0 = jnp.zeros((bsz, nh, n, n), jnp.float32)
    _, ys = lax.scan(step, s0, xs)
    return jnp.moveaxis(ys, 0, 1)


def setup_inputs(seed: int = 0) -> dict:
    key = jax.random.key(seed)
    ks = jax.random.split(key, 32)

    def nrm(k, shape, scale):
        return jax.random.normal(k, shape, jnp.float32) * scale

    L = DEPTH
    return {
        "x": nrm(ks[0], (BATCH, SEQ, D_MODEL), 1.0),
        "c": nrm(ks[1], (BATCH, D_MODEL), 1.0),
        "w_ada": nrm(ks[2], (L, D_MODEL, 6 * D_MODEL), 0.5 * D_MODEL ** -0.5),
        "b_ada": nrm(ks[3], (L, 6 * D_MODEL), 0.02),
        "g_norm_mix": 1.0 + nrm(ks[4], (L, D_MODEL), 0.05),
        "w_in": nrm(ks[5], (L, D_MODEL, IN_COLS), D_MODEL ** -0.5),
        "mu_shift": jax.random.uniform(ks[6], (L, RWKV_COLS), jnp.float32),
        "w0": jax.random.uniform(ks[7], (L, RWKV_WIDTH), jnp.float32, -4.0, 2.0),
        "w_decay_up": nrm(ks[8], (L, DECAY_LORA, RWKV_WIDTH), 0.5 * DECAY_LORA ** -0.5),
        "a0": nrm(ks[9], (L, RWKV_WIDTH), 0.1),
        "w_aaa_up": nrm(ks[10], (L, AAA_LORA, RWKV_WIDTH), 0.5 * AAA_LORA ** -0.5),
        "w_gate_up": nrm(ks[11], (L, GATE_LORA, RWKV_WIDTH), GATE_LORA ** -0.5),
        "k_k": 0.85 + nrm(ks[12], (L, RWKV_WIDTH), 0.05),
        "k_a": 1.0 + nrm(ks[13], (L, RWKV_WIDTH), 0.05),
        "r_k": nrm(ks[14], (L, RWKV_WIDTH), 0.1),
        "ln_x_w": 1.0 + nrm(ks[15], (L, RWKV_WIDTH), 0.05),
        "ln_x_b": nrm(ks[16], (L, RWKV_WIDTH), 0.02),
        "g_sb_out": 1.0 + nrm(ks[17], (L, SB_HEADS, SB_HEAD_DIM), 0.05),
        "w_out": nrm(ks[18], (L, MIX_WIDTH, D_MODEL), MIX_WIDTH ** -0.5),
        "g_norm_ffn": 1.0 + nrm(ks[19], (L, D_MODEL), 0.05),
        "w_up": nrm(ks[20], (L, D_MODEL, 2 * D_FF), D_MODEL ** -0.5),
        "conv_w": nrm(ks[21], (L, CONV_WIDTH, D_FF), CONV_WIDTH ** -0.5),
        "conv_b": nrm(ks[22], (L, D_FF), 0.02),
        "w_down": nrm(ks[23], (L, D_FF, D_MODEL), D_FF ** -0.5),
        "g_norm_final": 1.0 + nrm(ks[24], (D_MODEL,), 0.05),
    }


def reference(x, c, w_ada, b_ada, g_norm_mix, w_in, mu_shift, w0, w_decay_up, a0, w_aaa_up,
              w_gate_up, k_k, k_a, r_k, ln_x_w, ln_x_b, g_sb_out, w_out, g_norm_ffn, w_up,
              conv_w, conv_b, w_down, g_norm_final):
    bsz, seq, _ = x.shape
    for l in range(DEPTH):
        mod = jax.nn.silu(c) @ w_ada[l] + b_ada[l]
        sh_m, sc_m, gt_m, sh_f, sc_f, gt_f = jnp.split(mod, 6, axis=-1)

        h = modulate(rmsnorm(x, g_norm_mix[l]), sh_m, sc_m)
        proj = h @ w_in[l]
        sb_part = proj[..., :3 * SB_WIDTH]
        rw_part = proj[..., 3 * SB_WIDTH:]

        q, k, v = jnp.split(sb_part, 3, axis=-1)
        to_heads = lambda t: t.reshape(bsz, seq, SB_HEADS, SB_HEAD_DIM).transpose(0, 2, 1, 3)
        o_sb = stick_breaking_attention(to_heads(q), to_heads(k), to_heads(v))
        o_sb = rmsnorm(o_sb.transpose(0, 2, 1, 3), g_sb_out[l])
        o_sb = o_sb.reshape(bsz, seq, SB_WIDTH)

        rw_prev = jnp.pad(rw_part, ((0, 0), (1, 0), (0, 0)))[:, :-1]
        rw = rw_part + (rw_prev - rw_part) * mu_shift[l]
        r, kr, vr, xw, xa, xg = jnp.split(
            rw, [RWKV_WIDTH, 2 * RWKV_WIDTH, 3 * RWKV_WIDTH, 3 * RWKV_WIDTH + DECAY_LORA,
                 3 * RWKV_WIDTH + DECAY_LORA + AAA_LORA], axis=-1)
        w_log = -jax.nn.softplus(-(w0[l] + jnp.tanh(xw) @ w_decay_up[l])) - 0.5
        decay = jnp.exp(-jnp.exp(w_log.astype(jnp.float32)))
        a = jax.nn.sigmoid(a0[l] + xa @ w_aaa_up[l])
        g = jax.nn.sigmoid(xg) @ w_gate_up[l]
        kk = kr * k_k[l]
        k_mod = kr * (1 + (a - 1) * k_a[l])
        hd = lambda t: t.reshape(bsz, seq, RWKV_HEADS, RWKV_HEAD_SIZE).astype(jnp.float32)
        r_h, k_h, v_h, kk_h, a_h, w_h = hd(r), hd(k_mod), hd(vr), hd(kk), hd(a), hd(decay)
        kk_h = kk_h / jnp.maximum(jnp.linalg.norm(kk_h, axis=-1, keepdims=True), 1e-12)
        y = rwkv7_recurrence(r_h, w_h, k_h, v_h, kk_h, a_h)
        mu = jnp.mean(y, axis=-1, keepdims=True)
        var = jnp.mean(jnp.square(y - mu), axis=-1, keepdims=True)
        y = (y - mu) * lax.rsqrt(var + GN_EPS)
        y = y.reshape(bsz, seq, RWKV_WIDTH) * ln_x_w[l] + ln_x_b[l]
        r_k_h = r_k[l].reshape(RWKV_HEADS, RWKV_HEAD_SIZE).astype(jnp.float32)
        bonus = jnp.sum(r_h * k_h * r_k_h, axis=-1, keepdims=True) * v_h
        o_rw = ((y + bonus.reshape(bsz, seq, RWKV_WIDTH)) * g).astype(x.dtype)

        mixed = jnp.concatenate([o_sb, o_rw], axis=-1) @ w_out[l]
        x = x + gt_m[:, None, :] * mixed

        h = modulate(rmsnorm(x, g_norm_ffn[l]), sh_f, sc_f)
        u, val = jnp.split(h @ w_up[l], 2, axis=-1)
        u_pad = jnp.pad(u, ((0, 0), (CONV_WIDTH - 1, 0), (0, 0)))
        u_conv = conv_b[l] + conv_w[l, 0] * u_pad[:, 0:seq]
        for j in range(1, CONV_WIDTH):
            u_conv = u_conv + conv_w[l, j] * u_pad[:, j:j + seq]
        ff = (jax.nn.silu(u_conv) * val) @ w_down[l]
        x = x + gt_f[:, None, :] * ff

    return rmsnorm(x, g_norm_final)
```

```python
import numpy as np
from contextlib import ExitStack
import concourse.bass as bass
import concourse.mybir as mybir
from concourse.bass_utils import run_bass_kernel_spmd

F32 = mybir.dt.float32
BF16 = mybir.dt.bfloat16
F32R = mybir.dt.float32r
AF = mybir.ActivationFunctionType
ALU = mybir.AluOpType

D = 4096
S = 2048
KC = 32
INC = 12736
DFF = 11008
FC = 86
NORM_EPS = 1e-6
GN_EPS = 64e-5
BIGN = 77824


class Prog:
    ENG = ('pe', 'act', 'dve', 'pool', 'sp')

    def __init__(self, nc, stack):
        self.nc = nc
        self.stack = stack
        self.ops = {e: [] for e in self.ENG}
        self.sems = {}
        self.semcnt = {}
        self.waited = {e: {} for e in self.ENG}
        self.buf = {}
        for e in self.ENG:
            self._sem(('e', e))

    def _sem(self, key):
        if key not in self.sems:
            name = "s_" + "_".join(str(k) for k in key)
            self.sems[key] = self.stack.enter_context(self.nc.semaphore(name))
            self.semcnt[key] = 0
        return self.sems[key]

    def op(self, eng, fn, reads=(), writes=(), dma=None, sig=True, extra=(), same=True):
        needs = {}

        def need(ev):
            if ev is None:
                return
            k, v = ev
            if needs.get(k, 0) < v:
                needs[k] = v
        for k in reads:
            b = self.buf.get(k)
            if b:
                need(b['w'])
        for k in writes:
            b = self.buf.get(k)
            if b:
                need(b['w'])
                for ev in b['r'].items():
                    need(ev)
        for ev in extra:
            need(ev)
        waits = []
        for k, v in needs.items():
            if k == ('e', eng) and (eng == 'pe' or not same):
                continue
            if self.waited[eng].get(k, 0) >= v:
                continue
            self.waited[eng][k] = v
            waits.append((self.sems[k], v))
        if dma is not None:
            self._sem(dma)
            self.semcnt[dma] += 16
            ev = (dma, self.semcnt[dma])
            inc = (self.sems[dma], 16)
        else:
            k = ('e', eng)
            if sig:
                self.semcnt[k] += 1
                ev = (k, self.semcnt[k])
                inc = (self.sems[k], 1)
            else:
                ev = (k, self.semcnt[k] + 1)
                inc = None
        self.ops[eng].append((waits, fn, inc))
        for k in reads:
            b = self.buf.setdefault(k, {'w': None, 'r': {}})
            if b['r'].get(ev[0], 0) < ev[1]:
                b['r'][ev[0]] = ev[1]
        for k in writes:
            self.buf[k] = {'w': ev, 'r': {}}
        return ev

    def mm(self, out, lhsT, rhs, start=True, stop=True, **kw):
        return self.op('pe', lambda e: e.matmul(out, lhsT, rhs, start=start, stop=stop), **kw)

    def tr(self, out, in_, ident, **kw):
        return self.op('pe', lambda e: e.transpose(out, in_, ident), **kw)

    def act(self, out, in_, func, scale=1.0, bias=None, **kw):
        if bias is None:
            return self.op('act', lambda e: e.activation(out=out, in_=in_, func=func, scale=scale), **kw)
        return self.op('act', lambda e: e.activation(out=out, in_=in_, func=func, scale=scale, bias=bias), **kw)

    def cp(self, eng, out, in_, **kw):
        if eng == 'act':
            return self.op('act', lambda e: e.copy(out=out, in_=in_), **kw)
        return self.op(eng, lambda e: e.tensor_copy(out=out, in_=in_), **kw)

    def tt(self, eng, out, in0, in1, op, **kw):
        return self.op(eng, lambda e: e.tensor_tensor(out=out, in0=in0, in1=in1, op=op), **kw)

    def ts(self, eng, out, in0, s1, s2, op0, op1=None, **kw):
        if op1 is None:
            return self.op(eng, lambda e: e.tensor_scalar(out=out, in0=in0, scalar1=s1, scalar2=None, op0=op0), **kw)
        return self.op(eng, lambda e: e.tensor_scalar(out=out, in0=in0, scalar1=s1, scalar2=s2, op0=op0, op1=op1), **kw)

    def stt(self, eng, out, in0, scalar, in1, op0, op1, **kw):
        return self.op(eng, lambda e: e.scalar_tensor_tensor(out=out, in0=in0, scalar=scalar, in1=in1, op0=op0, op1=op1), **kw)

    def dma(self, eng, out, in_, sem, **kw):
        return self.op(eng, lambda e: e.dma_start(out=out, in_=in_), dma=sem, **kw)

    def barrier(self):
        evs = [(k, c) for k, c in self.semcnt.items() if c > 0]
        for e in self.ENG:
            self.op(e, None, extra=evs, sig=False)
        self.buf = {}

    def emit(self):
        nc = self.nc
        with nc.Block() as block:
            def run(e, lst):
                for waits, fn, inc in lst:
                    for s, v in waits:
                        e.wait_ge(s, v)
                    if fn is None:
                        continue
                    ins = fn(e)
                    if inc is not None:
                        ins.then_inc(inc[0], inc[1])

            @block.tensor
            def _(e):
                run(e, self.ops['pe'])

            @block.scalar
            def _(e):
                run(e, self.ops['act'])

            @block.vector
            def _(e):
                run(e, self.ops['dve'])

            @block.gpsimd
            def _(e):
                run(e, self.ops['pool'])

            @block.sync
            def _(e):
                run(e, self.ops['sp'])


def host_consts():
    c = {}
    c["ident"] = np.eye(128, dtype=np.float32)
    j = np.arange(128)[:, None]
    s = np.arange(128)[None, :]
    c["tri_incl"] = (j >= s).astype(np.float32)
    c["tri_c"] = (j < s).astype(np.float32)
    t = np.arange(512)[None, None, :]
    r = np.arange(4)[None, :, None]
    c["amask"] = (t > r * 128 + np.arange(128)[:, None, None]).astype(np.float32)
    c["blockones"] = ((j // 64) == (s // 64)).astype(np.float32)
    rm = np.ones((128, S), np.float32)
    rm[:, ::64] = 0.0
    c["resetmask"] = rm
    s64 = np.arange(64)[:, None]
    t64 = np.arange(64)[None, :]
    strict = (s64 < t64).astype(np.float32)
    incl = (s64 <= t64).astype(np.float32)
    m_si = np.concatenate([strict, incl], axis=1)
    m_lo = (s64 > t64).astype(np.float32)
    c["msk_si4"] = np.tile(m_si, (2, 4))
    c["msk_lo8"] = np.tile(m_lo, (2, 8))
    c["identx"] = np.tile(np.eye(64, dtype=np.float32), (2, 1))
    return c


def fm(v, p=128):
    v = np.asarray(v, np.float32)
    return np.ascontiguousarray(v.reshape(-1, p).T)


def build(stop_after=99, dbg=False, skip_rwkv=False):
    nc = bass.Bass("TRN2", target_bir_lowering=False)

    def din(name, shape, dt=F32):
        return nc.dram_tensor(name, list(shape), dt, kind="ExternalInput").ap()

    def dscr(name, shape, dt):
        return nc.dram_tensor(name, list(shape), dt, kind="Internal").ap()

    x = din("x", [S, D])
    c_fm = din("c_fm", [128, 32])
    w_ada = din("w_ada", [192, 128, D])
    b_ada_fm = din("b_ada_fm", [128, 192])
    gvec = din("gvec", [128, 3, 32])
    w_in = din("w_in", [D, INC])
    mu_fm = din("mu_fm", [128, 52])
    rwp = din("rwp", [128, 7, 16])
    w_dec = din("w_dec", [96, 2048])
    w_aaa = din("w_aaa", [96, 2048])
    w_gate = din("w_gate", [256, 2048])
    gsb_fm = din("gsb_fm", [128, 16])
    w_out = din("w_out", [D, D])
    w_up = din("w_up", [D, 2 * DFF])
    conv_fm = din("conv_fm", [128, 4, FC])
    w_down = din("w_down", [DFF, D])
    cst = {k: din("c_" + k, v.shape) for k, v in host_consts().items()}
    out = nc.dram_tensor("out", [S, D], F32, kind="ExternalOutput").ap()

    xT_d = dscr("xT_d", [D, S], F32)
    projT_d = dscr("projT_d", [12800, S], BF16)
    oT_d = dscr("oT_d", [D, S], BF16)
    x1T_d = dscr("x1T_d", [D, S], F32)
    actT_d = dscr("actT_d", [DFF, S], BF16)
    wdb_d = dscr("wdb_d", [32, 128, FC * 128], BF16)
    dbg_out = {}
    if dbg:
        dbg_out["d_projT"] = nc.dram_tensor("d_projT", [12800, S], BF16, kind="ExternalOutput").ap()
        dbg_out["d_oT"] = nc.dram_tensor("d_oT", [D, S], BF16, kind="ExternalOutput").ap()
        dbg_out["d_mod"] = nc.dram_tensor("d_mod", [128, 192], F32, kind="ExternalOutput").ap()
        dbg_out["d_x1T"] = nc.dram_tensor("d_x1T", [D, S], F32, kind="ExternalOutput").ap()

    with ExitStack() as st:
        P = Prog(nc, st)

        def sb(name, shape, dt=F32, stack=st):
            return stack.enter_context(nc.sbuf_tensor(name, list(shape), dt))

        PS = st.enter_context(nc.psum_tensor("PS", [128, 4096], F32))

        def bank(b):
            return PS[:, b * 512:(b + 1) * 512]

        big = sb("big", [128, BIGN], BF16)
        ident_f = sb("ident_f", [128, 128])
        ident_b = sb("ident_b", [128, 128], BF16)
        ones_b = sb("ones_b", [128, 128], BF16)
        tri_incl = sb("tri_incl", [128, 128])
        tri_c = sb("tri_c", [128, 128])
        amask = sb("amask", [128, 4, 512], BF16)
        blockones = sb("blockones", [128, 128])
        one11 = sb("one11", [1, 1])
        eps_col = sb("eps_col", [128, 1])
        gn_eps_col = sb("gn_eps_col", [128, 1])
        one_col = sb("one_col", [128, 1])
        mod_fm = sb("mod_fm", [128, 192])
        bada_sb = sb("bada_sb", [128, 192])
        gvec_sb = sb("gvec_sb", [128, 3, 32])
        gam_m = sb("gam_m", [128, 32])
        gam_f = sb("gam_f", [128, 32])
        gsb_sb = sb("gsb_sb", [128, 16])
        c_sb = sb("c_sb", [128, 32])
        sc_bf = sb("sc_bf", [128, 32], BF16)
        sh_m = mod_fm[:, 0:32]
        gt_m = mod_fm[:, 64:96]
        sh_f = mod_fm[:, 96:128]
        gt_f = mod_fm[:, 160:192]

        def D_(eng, out_, in_, key, dsem, reads=(), writes=()):
            return P.op(eng, lambda e: e.dma_start(out=out_, in_=in_), reads=reads, writes=writes, dma=dsem)

        D_('sp', ident_f[:], cst["ident"], None, ('a', 0), writes=['ident_f'])
        D_('pool', ident_b[:], cst["ident"], None, ('w', 0), writes=['ident_b'])
        D_('sp', tri_incl[:], cst["tri_incl"], None, ('a', 1), writes=['tri_incl'])
        D_('sp', tri_c[:], cst["tri_c"], None, ('a', 2), writes=['tri_c'])
        D_('pool', amask[:], cst["amask"], None, ('w', 1), writes=['amask'])
        D_('sp', blockones[:], cst["blockones"], None, ('a', 3), writes=['blockones'])
        D_('sp', bada_sb[:], b_ada_fm, None, ('a', 4), writes=['bada'])
        D_('sp', gvec_sb[:], gvec, None, ('a', 5), writes=['gvec'])
        D_('sp', gsb_sb[:], gsb_fm, None, ('s', 0), writes=['gsb'])
        D_('sp', c_sb[:], c_fm, None, ('s', 1), writes=['c_sb'])
        P.op('dve', lambda e: e.memset(ones_b[:], 1.0), writes=['ones_b'])
        P.op('dve', lambda e: e.memset(one11[:], 1.0), writes=['one11'])
        P.op('dve', lambda e: e.memset(eps_col[:], NORM_EPS), writes=['eps_col'])
        P.op('dve', lambda e: e.memset(gn_eps_col[:], GN_EPS), writes=['gn_eps_col'])
        P.op('dve', lambda e: e.memset(one_col[:], 1.0), writes=['one_col'])

        P.act(sc_bf[:], c_sb[:], AF.Silu, reads=['c_sb'], writes=['sc_bf'])
        hT = big[:, 0:65536].rearrange("p (k t) -> p k t", t=S)

        def ada_group(g, abuf):
            slot = g % 2
            P.dma('pool', abuf[slot].rearrange("p k c -> p (k c)"), w_ada[g], ('w', 4 + slot), writes=[('adaw', slot)])
            for k in range(32):
                P.mm(bank(7)[:, g:g + 1], abuf[slot][:, k, :], sc_bf[:, k:k + 1], start=(k == 0), stop=(k == 31),
                     reads=[('adaw', slot), 'sc_bf'], writes=['modps'], sig=(k == 31))

        def ada_flush():
            pass

        with ExitStack() as ph:
            xtm = sb("xtm", [128, D], F32, ph)
            abuf = [sb("abuf%d" % i, [128, 32, 128], BF16, ph)[:] for i in range(2)]
            xTb = big[:, 65536:73728].bitcast(F32)
            sq = [sb("sq%d" % i, [128, 512], BF16, ph) for i in range(2)]
            rt = sb("rt", [128, 128], F32, ph)
            rstd = sb("rstd", [128, 128], F32, ph)
            xT_dv = xT_d.rearrange("(k p) t -> p k t", p=128)
            import os
            VAR = os.environ.get("KVAR", "")
            if 'A' in VAR:
                for g in range(64):
                    ada_group(g, abuf)
                ada_flush()
            for tb in range(16):
                if 'A' not in VAR:
                    for g in range(4 * tb, 4 * tb + 4):
                        ada_group(g, abuf)
                P.dma('sp', xtm[:], x[tb * 128:(tb + 1) * 128, :], ('a', 0), writes=['xtm'])
                sbk = 4 + tb % 2
                for q in range(8):
                    bk = q % 4
                    for j in range(4):
                        k = q * 4 + j
                        P.tr(bank(bk)[:, j * 128:(j + 1) * 128], xtm[:, k * 128:(k + 1) * 128], ident_f[:],
                             reads=['xtm', 'ident_f'], writes=[('ps', bk)], sig=(j == 3))
                    P.cp('act', xTb[:, q * 512:(q + 1) * 512], bank(bk), reads=[('ps', bk)], writes=[('xTb', q)])
                    P.tt('dve', sq[q % 2][:], xTb[:, q * 512:(q + 1) * 512], xTb[:, q * 512:(q + 1) * 512], ALU.mult,
                         reads=[('xTb', q)], writes=[('sq', q % 2)])
                    for j in range(4):
                        P.mm(bank(sbk)[:, 0:128], ones_b[:], sq[q % 2][:, j * 128:(j + 1) * 128],
                             start=(q == 0 and j == 0), stop=(q == 7 and j == 3),
                             reads=[('sq', q % 2), 'ones_b'], writes=[('ps', sbk)], sig=(j == 3))
                P.act(rt[:], bank(sbk)[:, 0:128], AF.Sqrt, scale=1.0 / D, bias=eps_col[:, 0:1], reads=[('ps', sbk), 'eps_col'], writes=['rt'])
                P.op('dve', lambda e: e.reciprocal(out=rstd[:], in_=rt[:]), reads=['rt'], writes=['rstd'])
                P.tt('dve', hT[:, :, tb * 128:(tb + 1) * 128], xTb.rearrange("p (k t) -> p k t", t=128),
                     rstd[:].rearrange("p (o t) -> p o t", o=1).to_broadcast([128, 32, 128]), ALU.mult,
                     reads=[('xTb', q) for q in range(8)] + ['rstd'], writes=[('hT', tb)])
                P.dma('sp', xT_dv[:, :, tb * 128:(tb + 1) * 128], xTb.rearrange("p (k t) -> p k t", t=128), ('s', 0),
                      reads=[('xTb', q) for q in range(8)])
            ada_flush()
            P.tt('dve', mod_fm[:, 0:64], bank(7)[:, 0:64], bada_sb[:, 0:64], ALU.add, reads=['modps', 'bada'], writes=['mod'])
            P.stt('dve', gam_m[:], mod_fm[:, 32:64], 1.0, gvec_sb[:, 0, :], ALU.add, ALU.mult, reads=['mod', 'gvec'], writes=['gam_m'])
            for k in range(32):
                P.act(hT[:, k, :], hT[:, k, :], AF.Identity, scale=gam_m[:, k:k + 1], bias=sh_m[:, k:k + 1],
                      reads=[('hT', tb) for tb in range(16)] + ['gam_m', 'mod'], writes=[('hTk', k)], same=False)
            P.barrier()

        if stop_after >= 2:
            with ExitStack() as ph:
                wbuf = [sb("wbuf%d" % i, [128, 32, 128], BF16, ph) for i in range(3)]
                stage = [sb("stage%d" % i, [128, S], BF16, ph) for i in range(2)]
                abuf = [big[:, 65536 + i * 4096:65536 + (i + 1) * 4096].rearrange("p (k c) -> p k c", c=128) for i in range(2)]
                wv = w_in.rearrange("(k p) c -> p k c", p=128)
                ag = 64
                u = 0
                for ct in range(100):
                    cw = 128 if ct < 99 else 64
                    slot = ct % 3
                    P.dma('pool', wbuf[slot][:, :, 0:cw], wv[:, :, ct * 128:ct * 128 + cw], ('w', slot), writes=[('wb', slot)])
                    for tt in range(4):
                        bk = u % 6
                        u += 1
                        for k in range(32):
                            P.mm(bank(bk)[0:cw, :], wbuf[slot][:, k, 0:cw], hT[:, k, tt * 512:(tt + 1) * 512], start=(k == 0), stop=(k == 31),
                                 reads=[('wb', slot)], writes=[('ps', bk)], sig=(k == 31))
                        P.cp('act' if tt % 2 == 0 else 'dve', stage[ct % 2][0:cw, tt * 512:(tt + 1) * 512], bank(bk)[0:cw, :],
                             reads=[('ps', bk)], writes=[('stg', ct % 2, tt)])
                    P.dma('sp', projT_d[ct * 128:ct * 128 + cw, :], stage[ct % 2][0:cw, :], ('s', ct % 2),
                          reads=[('stg', ct % 2, tt) for tt in range(4)], writes=['projT'])
                    for _ in range(2 if ct < 28 else 1):
                        ada_group(ag, abuf)
                        ag += 1
                assert ag == 192
                ada_flush()
                P.tt('dve', mod_fm[:, 64:192], bank(7)[:, 64:192], bada_sb[:, 64:192], ALU.add, reads=['modps', 'bada'], writes=['mod'])
                P.stt('dve', gam_f[:], mod_fm[:, 128:160], 1.0, gvec_sb[:, 1, :], ALU.add, ALU.mult, reads=['mod', 'gvec'], writes=['gam_f'])
                if dbg:
                    P.dma('sp', dbg_out["d_mod"], mod_fm[:], ('s', 2), reads=['mod'])
                P.barrier()
                if dbg:
                    P.dma('sp', dbg_out["d_projT"][0:12736, :], projT_d[0:12736, :], ('s', 2), reads=['projT'])

        if stop_after >= 3:
            with ExitStack() as ph:
                NQ = 2048
                e_sb = [[sb("e_sb%d%d" % (i, j), [128, 512], F32, ph) for j in range(2)] for i in range(2)]
                L_sb = [[sb("L_sb%d%d" % (i, j), [128, 512], F32, ph) for j in range(2)] for i in range(2)]
                eg_sb = [[big[:, 32768 + (2 * i + j) * 1024:32768 + (2 * i + j + 1) * 1024].bitcast(F32) for j in range(2)] for i in range(2)]
                aw_sb = [[sb("aw_sb%d%d" % (i, j), [128, 512], BF16, ph) for j in range(2)] for i in range(2)]
                osb = [sb("osb%d" % i, [128, 512], F32, ph) for i in range(2)]
                osq = [big[:, 36864 + i * 512:36864 + (i + 1) * 512] for i in range(2)]
                ort = [sb("ort%d" % i, [128, 512], F32, ph) for i in range(2)]
                ostg = [sb("ostg%d" % i, [128, 512], BF16, ph) for i in range(2)]
                SCALE = float(1.0 / np.sqrt(128.0))
                wdv = w_down.rearrange("(k p) c -> p k c", p=128)

                def precast(first_extra):
                    pc_ev = []
                    for ct in range(32):
                        for (k0, nk) in ((0, 22), (22, 22), (44, 21), (65, 21)):
                            n = len(pc_ev)
                            ev = P.dma('pool', wdb_d[ct].rearrange("p (k c) -> p k c", c=128)[:, k0:k0 + nk, :],
                                       wdv[:, k0:k0 + nk, ct * 128:(ct + 1) * 128], ('w', 4 + n % 4),
                                       extra=([pc_ev[n - 4]] if n >= 4 else list(first_extra)))
                            pc_ev.append(ev)

                def hbuf(pp, i, j):
                    o = ((pp * 2 + i) * 3 + j) * NQ
                    return big[:, o:o + NQ]

                def vtm(i):
                    o = 24576 + i * NQ
                    return big[:, o:o + NQ].rearrange("p (b d) -> p b d", d=128)

                Ls = [[sb("Ls%d%d" % (i, j), [128, 512], F32, ph)[:] for j in range(2)] for i in range(2)]
                ones_f = sb("ones_r", [128, 128], F32R, ph)[:]
                tri_r = sb("tri_r", [128, 128], F32R, ph)[:]
                P.cp('dve', ones_f, one_col[:, 0:1].to_broadcast([128, 128]), reads=['one_col'], writes=['ones_f'])
                P.cp('dve', tri_r, tri_incl[:], reads=['tri_incl'], writes=['tri_r'])

                def stageA(pp, i, qt, idx):
                    kb = 4 * qt + 3 - idx
                    zb = 4 * i
                    sl = idx % 2
                    P.mm(bank(zb), hbuf(pp, i, 1)[:, kb * 128:(kb + 1) * 128], hbuf(pp, i, 0)[:, qt * 512:(qt + 1) * 512],
                         reads=[('qkv', pp, i, 0), ('qkv', pp, i, 1)], writes=[('ps', zb)])
                    P.act(e_sb[i][sl][:], bank(zb), AF.Exp, scale=SCALE, reads=[('ps', zb)], writes=[('e', i, sl)])
                    la, lk = lbuf(i, idx)
                    P.act(la.bitcast(F32R), e_sb[i][sl][:], AF.Ln, bias=one_col[:, 0:1], reads=[('e', i, sl), 'one_col'], writes=[lk])
                    if kb >= 4 * qt:
                        r = kb - 4 * qt
                        P.tt('dve', la.bitcast(F32R), la, amask[:, r, :], ALU.mult, reads=[lk, 'amask'], writes=[lk])
                        P.tt('dve', e_sb[i][sl][:], e_sb[i][sl][:], amask[:, r, :], ALU.mult, reads=[('e', i, sl), 'amask'], writes=[('e', i, sl)])

                def lbuf(i, idx):
                    if idx == 0:
                        return Ls[i][0][:], ('Ls', i, 0)
                    return L_sb[i][idx % 2][:], ('L', i, idx % 2)

                def lprev(i, idx):
                    return Ls[i][(idx - 1) % 2][:], ('Ls', i, (idx - 1) % 2)

                def cum(i, idx):
                    cb = 4 * i + 1 + idx % 2
                    la, lk = lbuf(i, idx)
                    if idx == 0:
                        P.mm(bank(cb), tri_r, la.bitcast(F32R), start=True, stop=True, reads=[lk, 'tri_r'], writes=[('ps', cb)])
                    else:
                        pa, pk = lprev(i, idx)
                        P.mm(bank(cb), ones_f, pa.bitcast(F32R), start=True, stop=False, reads=[pk, 'ones_f'], writes=[('ps', cb)], sig=False)
                        P.mm(bank(cb), tri_r, la.bitcast(F32R), start=False, stop=True, reads=[lk, 'tri_r', pk], writes=[('ps', cb)])

                def lsum(i, idx):
                    pa, pk = lprev(i, idx)
                    la, lk = lbuf(i, idx)
                    P.tt('dve', Ls[i][idx % 2].bitcast(F32R), pa, la, ALU.add, reads=[pk, lk], writes=[('Ls', i, idx % 2)])

                def egf(i, idx):
                    cb = 4 * i + 1 + idx % 2
                    sl = idx % 2
                    P.act(eg_sb[i][sl][:], bank(cb), AF.Exp, scale=-1.0, reads=[('ps', cb)], writes=[('eg', i, sl)])

                def awf(i, idx):
                    sl = idx % 2
                    P.tt('dve', aw_sb[i][sl][:], e_sb[i][sl][:], eg_sb[i][sl][:], ALU.mult, reads=[('e', i, sl), ('eg', i, sl)], writes=[('aw', i, sl)])

                def avf(i, qt, idx, last):
                    kb = 4 * qt + 3 - idx
                    ob = 4 * i + 3
                    sl = idx % 2
                    P.mm(bank(ob), vtm(i)[:, kb, :], aw_sb[i][sl][:], start=(idx == 0), stop=last,
                         reads=[('aw', i, sl), ('vtm', i, 0), ('vtm', i, 1)], writes=[('ps', ob)])

                def final1(i, h, qt):
                    ob = 4 * i + 3
                    nb_ = 4 * i + 2
                    P.cp('act', osb[i][:], bank(ob), reads=[('ps', ob)], writes=[('osb', i)])
                    P.tt('dve', osq[i][:], osb[i][:], osb[i][:], ALU.mult, reads=[('osb', i)], writes=[('osq', i)])
                    P.mm(bank(nb_), ones_b[:], osq[i][:], reads=[('osq', i), 'ones_b'], writes=[('ps', nb_)])

                def final2(i, h, qt):
                    nb_ = 4 * i + 2
                    P.act(ort[i][:], bank(nb_), AF.Ln, scale=1.0 / 128.0, bias=eps_col[:, 0:1], reads=[('ps', nb_), 'eps_col'], writes=[('ort', i)])
                    P.act(ort[i][:], ort[i][:], AF.Exp, scale=-0.5, reads=[('ort', i)], writes=[('ort', i)])
                    P.stt('dve', ostg[i][:], osb[i][:], gsb_sb[:, h:h + 1], ort[i][:], ALU.mult, ALU.mult,
                          reads=[('osb', i), ('ort', i), 'gsb'], writes=[('ostg', i)])
                    P.dma('sp', oT_d[h * 128:(h + 1) * 128, qt * 512:(qt + 1) * 512], ostg[i][:], ('s', i), reads=[('ostg', i)], writes=['oT'])

                def load_qkv(hp):
                    pp = hp % 2
                    evs = []
                    for i in range(2):
                        h = 2 * hp + i
                        for j in range(3):
                            evs.append(P.dma('sp', hbuf(pp, i, j), projT_d[j * 2048 + h * 128:j * 2048 + (h + 1) * 128, :],
                                             ('a', (pp * 2 + i) * 3 + j), writes=[('qkv', pp, i, j)]))
                    return evs

                pend = []
                precast(load_qkv(0))
                started = False
                for hp in range(8):
                    pp = hp % 2
                    heads = (2 * hp, 2 * hp + 1)
                    for i in range(2):
                        for half in range(2):
                            bk = 4 * i + half
                            bkb = bank(bk).bitcast(BF16)
                            for t8 in range(8):
                                tb = half * 8 + t8
                                P.tr(bkb[:, t8 * 128:(t8 + 1) * 128], hbuf(pp, i, 2)[:, tb * 128:(tb + 1) * 128], ident_b[:],
                                     reads=[('qkv', pp, i, 2), 'ident_b'], writes=[('ps', bk)], sig=(t8 == 7))
                            dst = big[:, 24576 + i * NQ + half * 1024:24576 + i * NQ + (half + 1) * 1024]
                            P.cp('dve' if half else 'act', dst, bkb[:, 0:1024], reads=[('ps', bk)], writes=[('vtm', i, half)])
                    for qt in range(4):
                        n_it = 4 * qt + 4
                        if qt == 3 and hp + 1 < 8:
                            load_qkv(hp + 1)
                        if not started:
                            for i in range(2):
                                stageA(pp, i, qt, 0)
                        started = False
                        for idx in range(n_it):
                            if idx + 1 < n_it:
                                for i in range(2):
                                    stageA(pp, i, qt, idx + 1)
                            if idx == 0:
                                for f in pend:
                                    f()
                                pend = []
                            for i in range(2):
                                cum(i, idx)
                            if idx == n_it - 1 and qt + 1 < 4:
                                for i in range(2):
                                    stageA(pp, i, qt + 1, 0)
                                started = True
                            if idx >= 1:
                                for i in range(2):
                                    avf(i, qt, idx - 1, False)
                            for i in range(2):
                                egf(i, idx)
                            if 1 <= idx < n_it - 1:
                                for i in range(2):
                                    lsum(i, idx)
                            for i in range(2):
                                awf(i, idx)
                        for i in range(2):
                            avf(i, qt, n_it - 1, True)
                        for i in range(2):
                            final1(i, heads[i], qt)
                        for i in range(2):
                            pend.append((lambda i=i, h=heads[i], qt=qt: final2(i, h, qt)))
                for f in pend:
                    f()
                P.barrier()

        if stop_after >= 4 and not skip_rwkv:
            with ExitStack() as ph:
                NEG_E = -float(np.exp(-0.5))
                _off = [0]

                def carve(nbf16):
                    o = _off[0]
                    _off[0] += (nbf16 + 1) // 2 * 2
                    assert _off[0] <= BIGN, _off[0]
                    return big[:, o:o + nbf16]

                def cf32(n):
                    return carve(2 * n).bitcast(F32)
                lds = [carve(2052)[:, 0:2049] for _ in range(3)]
                T1, T2, T3, T4, T5 = [cf32(S) for _ in range(5)]
                LW = cf32(S)
                LG = cf32(S)
                vb = carve(S)
                AR = carve(2 * S).rearrange("p (c t) -> p c t", t=128)
                BBAR = carve(S)
                KBAR = carve(S)
                oBT = _off[0]
                BT = carve(S)
                KT = carve(S)
                TMPK = big[:, oBT:oBT + 2 * S].bitcast(F32)
                Vtm = carve(S)
                BtT = carve(S)
                KtT = carve(S)
                AB = carve(2 * S).rearrange("p (c t) -> p c t", t=128)
                AK = carve(2 * S).rearrange("p (c t) -> p c t", t=128)
                oPa = _off[0]
                Pa, PTa = carve(S), carve(S)
                PA2 = big[:, oPa:oPa + 2 * S].bitcast(F32)
                oPb = _off[0]
                Pb, PTb, Tm = carve(S), carve(S), carve(S)
                BON = big[:, oPb:oPb + 2 * S].bitcast(F32)
                Ysb = LW
                ostg4 = carve(S)
                txw = sb("txw", [128, S], BF16, ph)
                xa_m = sb("xa_m", [128, S], BF16, ph)
                sxg = sb("sxg", [128, 2, S], BF16, ph)
                w_dec_b = sb("w_dec_b", [128, S], BF16, ph)
                w_aaa_b = sb("w_aaa_b", [128, S], BF16, ph)
                w_gate_b = sb("w_gate_b", [128, 2, S], BF16, ph)
                rmask = sb("rmask", [128, S], BF16, ph)
                msk_si4 = sb("msk_si4", [128, 512], BF16, ph)
                msk_lo8 = sb("msk_lo8", [128, 512], BF16, ph)
                identx = sb("identx", [128, 64], BF16, ph)
                rwp_sb = sb("rwp_sb", [128, 7, 16], F32, ph)
                mu_sb = sb("mu_sb", [128, 52], F32, ph)
                GC = sb("GC", [128, 32], F32, ph)
                Zf = sb("Zf", [128, 64], F32, ph)
                Zb = sb("Zb", [128, 64], BF16, ph)
                Xs = sb("Xs", [128, 64], BF16, ph)
                Us = sb("Us", [128, 64], BF16, ph)
                tiny = sb("tiny", [128, 1], F32, ph)

                def v3(ap, t=64):
                    return ap.rearrange("p (c t) -> p c t", t=t)
                P.dma('pool', w_dec_b[0:96, :], w_dec, ('w', 0), writes=['w_dec_b'])
                P.dma('pool', w_aaa_b[0:96, :], w_aaa, ('w', 1), writes=['w_aaa_b'])
                P.dma('pool', w_gate_b[:], w_gate.rearrange("(j p) c -> p j c", p=128), ('w', 2), writes=['w_gate_b'])
                P.dma('pool', rmask[:], cst["resetmask"], ('w', 3), writes=['rmask'])
                P.dma('pool', msk_si4[:], cst["msk_si4"], ('s', 1), writes=['msk_si4'])
                P.dma('pool', msk_lo8[:], cst["msk_lo8"], ('s', 2), writes=['msk_lo8'])
                P.dma('pool', identx[:], cst["identx"], ('s', 3), writes=['identx'])
                P.dma('sp', rwp_sb[:], rwp, ('a', 0), writes=['rwp'])
                P.dma('sp', mu_sb[:], mu_fm, ('a', 1), writes=['mu'])
                for li in range(3):
                    P.op('dve', (lambda li=li: lambda e: e.memset(lds[li][:, 0:1], 0.0))(), writes=[('ld0', li)])
                P.op('dve', lambda e: e.memset(tiny[:], 1e-24), writes=['tiny'])

                def load_mix(row0, nrow, mucol, dst, dkey, li=0, eng='dve'):
                    ld = lds[li]
                    P.dma('sp', ld[0:nrow, 1:2049], projT_d[row0:row0 + nrow, :], ('a', 2 + li), writes=[('ld', li)])
                    P.tt(eng, T1[0:nrow, :], ld[0:nrow, 0:2048], ld[0:nrow, 1:2049], ALU.subtract, reads=[('ld', li), ('ld0', li)], writes=['T1'])
                    if eng == 'pool':
                        P.tt(eng, T1[0:nrow, :], T1[0:nrow, :], mu_sb[0:nrow, mucol:mucol + 1].to_broadcast([nrow, S]), ALU.mult, reads=['T1', 'mu'], writes=['T1'])
                        P.tt(eng, dst, T1[0:nrow, :], ld[0:nrow, 1:2049], ALU.add, reads=['T1', ('ld', li)], writes=[dkey])
                    else:
                        P.stt(eng, dst, T1[0:nrow, :], mu_sb[0:nrow, mucol:mucol + 1], ld[0:nrow, 1:2049], ALU.mult, ALU.add,
                              reads=['T1', ('ld', li), 'mu'], writes=[dkey])

                def mixes(f, eng):
                    load_mix(10240 + f * 128, 128, 32 + f, vb, 'vb', 0, eng)
                    load_mix(8192 + f * 128, 128, 16 + f, T3, 'T3', 1, eng)
                    load_mix(6144 + f * 128, 128, f, T2, 'T2', 2, eng)
                load_mix(12288, 96, 48, T2[0:96, :], 'T2')
                P.act(txw[0:96, :], T2[0:96, :], AF.Tanh, reads=['T2'], writes=['txw'])
                load_mix(12384, 96, 49, xa_m[0:96, :], 'xa_m')
                for j in range(2):
                    load_mix(12480 + j * 128, 128, 50 + j, T2[:, :], 'T2')
                    P.act(sxg[:, j, :], T2[:, :], AF.Sigmoid, reads=['T2'], writes=[('sxg', j)])

                def blocksum(src, skey):
                    for b in range(4):
                        P.mm(bank(b), blockones[:], src[:, b * 512:(b + 1) * 512], reads=[skey, 'blockones'], writes=[('ps', b)])
                PSA = PS[:, 0:2048]
                PSB = PS[:, 2048:4096]
                psk4 = [('ps', b) for b in range(4)]
                psk8 = [('ps', b) for b in range(4, 8)]
                mixes(0, 'dve')
                for f in range(16):
                    col = lambda q: rwp_sb[:, q, f:f + 1]
                    for b in range(4):
                        P.mm(bank(b), w_dec_b[0:96, f * 128:(f + 1) * 128], txw[0:96, b * 512:(b + 1) * 512],
                             reads=['w_dec_b', 'txw'], writes=[('ps', b)])
                    for b in range(4):
                        P.mm(bank(4 + b), w_aaa_b[0:96, f * 128:(f + 1) * 128], xa_m[0:96, b * 512:(b + 1) * 512],
                             reads=['w_aaa_b', 'xa_m'], writes=[('ps', 4 + b)])
                    P.act(LW, PSA, AF.Sigmoid, bias=col(0), reads=psk4 + ['rwp'], writes=['LW'])
                    P.act(T4, PSB, AF.Sigmoid, bias=col(1), reads=psk8 + ['rwp'], writes=['T4'])
                    P.ts('dve', T1, T3, col(2), None, ALU.mult, reads=['T3', 'rwp'], writes=['T1'])
                    P.tt('dve', T5, T1, T1, ALU.mult, reads=['T1'], writes=['T5'])
                    blocksum(T5, 'T5')
                    P.ts('dve', LW, LW, NEG_E, None, ALU.mult, reads=['LW'], writes=['LW'])
                    P.op('dve', lambda e: e.tensor_tensor_scan(out=LG, data0=rmask[:], data1=LW, initial=0.0, op0=ALU.mult, op1=ALU.add),
                         reads=['LW', 'rmask'], writes=['LG'])
                    P.ts('dve', TMPK, T4, -1.0, col(3), ALU.add, ALU.mult, reads=['T4', 'rwp'], writes=['BT', 'KT'])
                    P.stt('dve', T3, TMPK, 1.0, T3, ALU.add, ALU.mult, reads=['BT', 'KT', 'T3'], writes=['T3'])
                    P.ts('dve', T5, PSA, tiny[:, 0:1], None, ALU.max, reads=psk4 + ['tiny'], writes=['T5'])
                    P.act(T5, T5, AF.Ln, reads=['T5'], writes=['T5'])
                    P.act(T5, T5, AF.Exp, scale=-0.5, reads=['T5'], writes=['T5'])
                    lgc = v3(LG)[:, :, 63:64]
                    P.tt('dve', LW, LG, LW, ALU.subtract, reads=['LG', 'LW'], writes=['LW'])
                    P.tt('dve', v3(BON), v3(LG), lgc.to_broadcast([128, 32, 64]), ALU.subtract, reads=['LG'], writes=['BON', 'Pb', 'PTb'])
                    P.act(PA2, LG, AF.Exp, reads=['LG'], writes=['Pa', 'PTa'])
                    P.act(GC[:].rearrange("p (c o) -> p c o", o=1), lgc, AF.Exp, reads=['LG'], writes=['GC'])
                    P.act(LW, LW, AF.Exp, reads=['LW'], writes=['LW'])
                    P.act(BON, BON, AF.Exp, scale=-1.0, reads=['BON', 'Pb', 'PTb'], writes=['BON', 'Pb', 'PTb'])
                    P.act(LG, LG, AF.Exp, scale=-1.0, reads=['LG'], writes=['LG'])
                    P.tt('dve', AR[:, :, 64:128], v3(T2), v3(PA2), ALU.mult, reads=['T2', 'Pa', 'PTa'], writes=['ARr'])
                    P.stt('dve', T2, T2, col(4), T3, ALU.mult, ALU.mult, reads=['T2', 'T3', 'rwp'], writes=['T2'])
                    for b in range(4):
                        P.mm(bank(4 + b), blockones[:], T2[:, b * 512:(b + 1) * 512], reads=['T2', 'blockones'], writes=[('ps', 4 + b)])
                    P.cp('act', T2, PSB, reads=psk8, writes=['T2'])
                    P.tt('dve', T1, T1, T5, ALU.mult, reads=['T1', 'T5'], writes=['T1'])
                    P.tt('dve', T4, T1, T4, ALU.mult, reads=['T1', 'T4'], writes=['T4'])
                    P.stt('dve', AR[:, :, 0:64], v3(T1), -1.0, v3(LW), ALU.mult, ALU.mult, reads=['T1', 'LW'], writes=['ARa'])
                    P.tt('dve', BBAR, T4, LG, ALU.mult, reads=['T4', 'LG'], writes=['BBAR'])
                    P.tt('dve', KBAR, T3, LG, ALU.mult, reads=['T3', 'LG'], writes=['KBAR'])
                    P.tt('dve', BT, T4, BON, ALU.mult, reads=['T4', 'BON', 'Pb', 'PTb'], writes=['BT'])
                    P.tt('dve', KT, T3, BON, ALU.mult, reads=['T3', 'BON', 'Pb', 'PTb'], writes=['KT'])
                    for qi, (src, dst, skey, dkey) in enumerate(((vb, Vtm, 'vb', 'Vtm'), (BT, BtT, 'BT', 'BtT'), (KT, KtT, 'KT', 'KtT'))):
                        for half in range(2):
                            bk = (qi * 2 + half) % 8
                            bkb = bank(bk).bitcast(BF16)
                            for c16 in range(16):
                                c = half * 16 + c16
                                for hh in range(2):
                                    ps_ = slice(hh * 64, (hh + 1) * 64)
                                    P.tr(bkb[ps_, c16 * 64:(c16 + 1) * 64], src[ps_, c * 64:(c + 1) * 64], ident_b[ps_, ps_],
                                         reads=[skey, 'ident_b'], writes=[('ps', bk)], sig=(c16 == 15 and hh == 1))
                            P.cp('act' if half else 'dve', dst[:, half * 1024:(half + 1) * 1024], bkb[:, 0:1024], reads=[('ps', bk)], writes=[dkey])
                    for (lhs, dst, lkey, dkey) in ((BBAR, AB, 'BBAR', 'AB'), (KBAR, AK, 'KBAR', 'AK')):
                        for c in range(32):
                            bk = c // 4
                            for hh in range(2):
                                ps_ = slice(hh * 64, (hh + 1) * 64)
                                P.mm(bank(bk)[ps_, (c % 4) * 128:(c % 4 + 1) * 128], lhs[ps_, c * 64:(c + 1) * 64], AR[ps_, c, :],
                                     reads=[lkey, 'ARa', 'ARr'], writes=[('ps', bk)], sig=(c % 4 == 3 and hh == 1))
                            if c % 4 == 3:
                                P.tt('dve', dst[:, c - 3:c + 1, :], bank(bk).rearrange("p (c t) -> p c t", t=128),
                                     msk_si4[:].rearrange("p (c t) -> p c t", t=128), ALU.mult, reads=[('ps', bk), 'msk_si4'], writes=[dkey])
                    for c in range(32):
                        bk = c // 8
                        for hh in range(2):
                            ps_ = slice(hh * 64, (hh + 1) * 64)
                            P.mm(bank(bk)[ps_, (c % 8) * 64:(c % 8 + 1) * 64], AR[ps_, c, 0:64], BBAR[ps_, c * 64:(c + 1) * 64],
                                 reads=['BBAR', 'ARa'], writes=[('ps', bk)], sig=(c % 8 == 7 and hh == 1))
                        if c % 8 == 7:
                            P.tt('dve', PTa[:, (c - 7) * 64:(c + 1) * 64], bank(bk), msk_lo8[:], ALU.mult, reads=[('ps', bk), 'msk_lo8'], writes=['PTa'])
                    P.tt('dve', v3(Tm), AB[:, :, 0:64], identx[:].rearrange("p (o t) -> p o t", o=1).to_broadcast([128, 32, 64]), ALU.add,
                         reads=['AB', 'identx'], writes=['Tm'])
                    Pc = None
                    PTc, PTk = PTa, 'PTa'
                    for lvl in range(1, 6):
                        Pn, PTn = (Pb, PTb) if lvl % 2 else (Pa, PTa)
                        Pnk, PTnk = ('Pb', 'PTb') if lvl % 2 else ('Pa', 'PTa')
                        for c in range(32):
                            for hh in range(2):
                                ps_ = slice(hh * 64, (hh + 1) * 64)
                                Pcur = AB[ps_, c, 0:64] if Pc is None else Pc[ps_, c * 64:(c + 1) * 64]
                                PTcur = PTc[ps_, c * 64:(c + 1) * 64]
                                pk = 'AB' if Pc is None else Pck
                                last = (c % 8 == 7 and hh == 1)
                                if lvl < 5:
                                    P.mm(bank(c // 8)[ps_, (c % 8) * 64:(c % 8 + 1) * 64], PTcur, Pcur, reads=[pk, PTk], writes=[('ps', c // 8)], sig=last)
                                P.mm(bank(4 + c // 8)[ps_, (c % 8) * 64:(c % 8 + 1) * 64], Pcur, PTcur, reads=[pk, PTk], writes=[('ps', 4 + c // 8)], sig=last)
                            if c % 8 == 7:
                                g8 = c // 8
                                if lvl < 5:
                                    P.cp('act', Pn[:, g8 * 512:(g8 + 1) * 512], bank(g8), reads=[('ps', g8)], writes=[Pnk])
                                P.cp('dve', PTn[:, g8 * 512:(g8 + 1) * 512], bank(4 + g8), reads=[('ps', 4 + g8)], writes=[PTnk])
                        for c in range(32):
                            for hh in range(2):
                                ps_ = slice(hh * 64, (hh + 1) * 64)
                                P.mm(bank(c // 8)[ps_, (c % 8) * 64:(c % 8 + 1) * 64], PTn[ps_, c * 64:(c + 1) * 64], Tm[ps_, c * 64:(c + 1) * 64],
                                     reads=[PTnk, 'Tm'], writes=[('ps', c // 8)], sig=(c % 8 == 7 and hh == 1))
                            if c % 8 == 7:
                                g8 = c // 8
                                P.tt('dve', Tm[:, g8 * 512:(g8 + 1) * 512], bank(g8), Tm[:, g8 * 512:(g8 + 1) * 512], ALU.add,
                                     reads=[('ps', g8), 'Tm'], writes=['Tm'])
                        Pc, Pck = Pn, Pnk
                        PTc, PTk = PTn, PTnk
                    P.tt('dve', BON, T2, vb, ALU.mult, reads=['T2', 'vb'], writes=['BON', 'Pb', 'PTb'])
                    if f + 1 < 16:
                        mixes(f + 1, 'pool')
                    P.op('dve', lambda e: e.memset(Zf[:], 0.0), writes=['Zf'])
                    P.op('dve', lambda e: e.memset(Zb[:], 0.0), writes=['Zb'])
                    for c in range(32):
                        yb = 3 + (c // 8) % 2
                        for hh in range(2):
                            ps_ = slice(hh * 64, (hh + 1) * 64)
                            P.mm(bank(0)[ps_, 0:64], AR[ps_, c, 0:64], Zb[ps_, :], start=True, stop=False, reads=['ARa', 'Zb'], writes=[('ps', 0)], sig=False)
                            P.mm(bank(0)[ps_, 0:64], AK[ps_, c, 0:64], Vtm[ps_, c * 64:(c + 1) * 64], start=False, stop=True,
                                 reads=['AK', 'Vtm'], writes=[('ps', 0)], sig=(hh == 1))
                        P.cp('act', Xs[:], bank(0)[:, 0:64], reads=[('ps', 0)], writes=['Xs'])
                        for hh in range(2):
                            ps_ = slice(hh * 64, (hh + 1) * 64)
                            P.mm(bank(1)[ps_, 0:64], Tm[ps_, c * 64:(c + 1) * 64], Xs[ps_, :], reads=['Tm', 'Xs'], writes=[('ps', 1)], sig=(hh == 1))
                        P.cp('dve', Us[:], bank(1)[:, 0:64], reads=[('ps', 1)], writes=['Us'])
                        for hh in range(2):
                            ps_ = slice(hh * 64, (hh + 1) * 64)
                            yo = bank(yb)[ps_, (c % 8) * 64:(c % 8 + 1) * 64]
                            P.mm(yo, Zb[ps_, :], AR[ps_, c, 64:128], start=True, stop=False, reads=['Zb', 'ARr'], writes=[('ps', yb)], sig=False)
                            P.mm(yo, Us[ps_, :], AB[ps_, c, 64:128], start=False, stop=False, reads=['Us', 'AB'], writes=[('ps', yb)], sig=False)
                            P.mm(yo, Vtm[ps_, c * 64:(c + 1) * 64], AK[ps_, c, 64:128], start=False, stop=True, reads=['Vtm', 'AK'], writes=[('ps', yb)], sig=(hh == 1))
                        for hh in range(2):
                            ps_ = slice(hh * 64, (hh + 1) * 64)
                            P.mm(bank(2)[ps_, 0:64], BtT[ps_, c * 64:(c + 1) * 64], Us[ps_, :], start=True, stop=False, reads=['BtT', 'Us'], writes=[('ps', 2)], sig=False)
                            P.mm(bank(2)[ps_, 0:64], KtT[ps_, c * 64:(c + 1) * 64], Vtm[ps_, c * 64:(c + 1) * 64], start=False, stop=True,
                                 reads=['KtT', 'Vtm'], writes=[('ps', 2)], sig=(hh == 1))
                        P.stt('dve', Zb[:], Zf[:], GC[:, c:c + 1], bank(2)[:, 0:64], ALU.mult, ALU.add, reads=['Zf', 'GC', ('ps', 2)], writes=['Zb'])
                        P.stt('dve', Zf[:], Zf[:], GC[:, c:c + 1], bank(2)[:, 0:64], ALU.mult, ALU.add, reads=['Zf', 'GC', ('ps', 2)], writes=['Zf'])
                        if c % 8 == 7:
                            g8 = c // 8
                            P.cp('act', Ysb[:, g8 * 512:(g8 + 1) * 512], bank(yb), reads=[('ps', yb)], writes=['LW'])
                    blocksum(Ysb, 'LW')
                    P.stt('dve', T5, PSA, -1.0 / 64.0, Ysb, ALU.mult, ALU.add, reads=psk4 + ['LW'], writes=['T5'])
                    P.tt('dve', T1, T5, T5, ALU.mult, reads=['T5'], writes=['T1'])
                    blocksum(T1, 'T1')
                    P.act(T1, PSA, AF.Ln, scale=1.0 / 64.0, bias=gn_eps_col[:, 0:1], reads=psk4 + ['gn_eps_col'], writes=['T1'])
                    P.act(T1, T1, AF.Exp, scale=-0.5, reads=['T1'], writes=['T1'])
                    P.tt('dve', T5, T5, T1, ALU.mult, reads=['T5', 'T1'], writes=['T5'])
                    P.act(T5, T5, AF.Identity, scale=col(5), bias=col(6), reads=['T5', 'rwp'], writes=['T5'])
                    P.tt('dve', T5, T5, BON, ALU.add, reads=['T5', 'BON', 'Pb', 'PTb'], writes=['T5'])
                    for b in range(4):
                        for j in range(2):
                            P.mm(bank(4 + b), w_gate_b[:, j, f * 128:(f + 1) * 128], sxg[:, j, b * 512:(b + 1) * 512], start=(j == 0), stop=(j == 1),
                                 reads=['w_gate_b', ('sxg', 0), ('sxg', 1)], writes=[('ps', 4 + b)], sig=(j == 1))
                    P.tt('dve', ostg4, T5, PSB, ALU.mult, reads=['T5'] + psk8, writes=['ostg4'])
                    P.dma('sp', oT_d[2048 + f * 128:2048 + (f + 1) * 128, :], ostg4, ('s', 0), reads=['ostg4'], writes=['oT'])
                P.barrier()
        elif stop_after >= 5:
            with ExitStack() as ph:
                zt = sb("zt", [128, S], BF16, ph)
                P.op('dve', lambda e: e.memset(zt[:], 0.0), writes=['zt'])
                for f in range(16):
                    P.dma('sp', oT_d[2048 + f * 128:2048 + (f + 1) * 128, :], zt[:], ('s', f % 2), reads=['zt'], writes=['oT'])
                P.barrier()

        if stop_after >= 5:
            oTb = big[:, 0:65536].rearrange("p (k t) -> p k t", t=S)
            with ExitStack() as ph:
                wbuf = [sb("wobuf%d" % i, [128, 32, 128], BF16, ph) for i in range(3)]
                xt_sb = [sb("xt_sb%d" % i, [128, 512], F32, ph) for i in range(2)]
                x1s = [sb("x1s%d" % i, [128, 512], F32, ph) for i in range(2)]
                sqb = [sb("sqb%d" % i, [128, 512], BF16, ph) for i in range(2)]
                ssq = sb("ssq", [128, S], F32, ph)
                rstd2 = ssq
                tmp5 = x1s
                oT_dv = oT_d.rearrange("(k p) t -> p k t", p=128)
                for q in range(8):
                    P.dma('sp', oTb[:, q * 4:(q + 1) * 4, :], oT_dv[:, q * 4:(q + 1) * 4, :], ('a', q), writes=[('oTb', q)])
                P.op('dve', lambda e: e.memset(ssq[:], 0.0), writes=['ssq'])
                wv = w_out.rearrange("(k p) c -> p k c", p=128)
                pend = None
                u = 0
                for ct in range(32):
                    slot = ct % 3
                    P.dma('pool', wbuf[slot][:], wv[:, :, ct * 128:(ct + 1) * 128], ('w', slot), writes=[('wb', slot)])
                    for tt in range(4):
                        bk = u % 4
                        sbk = 4 + u % 4
                        us = u % 2
                        P.dma('sp', xt_sb[us][:], xT_d[ct * 128:(ct + 1) * 128, tt * 512:(tt + 1) * 512], ('a', 8 + us), writes=[('xt', us)])
                        for k in range(32):
                            P.mm(bank(bk), wbuf[slot][:, k, :], oTb[:, k, tt * 512:(tt + 1) * 512], start=(k == 0), stop=(k == 31),
                                 reads=[('wb', slot)] + ([('oTb', k // 4)] if ct == 0 else []), writes=[('ps', bk)], sig=(k == 31))
                        if pend is not None:
                            pend()
                        P.stt('dve', x1s[us][:], bank(bk), gt_m[:, ct:ct + 1], xt_sb[us][:], ALU.mult, ALU.add,
                              reads=[('ps', bk), ('xt', us), 'mod'], writes=[('x1s', us)])
                        P.act(sqb[us][:], x1s[us][:], AF.Square, reads=[('x1s', us)], writes=[('sqb', us)])
                        P.dma('sp', x1T_d[ct * 128:(ct + 1) * 128, tt * 512:(tt + 1) * 512], x1s[us][:], ('s', us), reads=[('x1s', us)], writes=['x1T'])

                        def mk(us=us, sbk=sbk, tt=tt):
                            def f():
                                P.mm(bank(sbk), ones_b[:], sqb[us][:], reads=[('sqb', us), 'ones_b'], writes=[('ps', sbk)])
                                P.tt('dve', ssq[:, tt * 512:(tt + 1) * 512], bank(sbk), ssq[:, tt * 512:(tt + 1) * 512], ALU.add,
                                     reads=[('ps', sbk), 'ssq'], writes=['ssq'])
                            return f
                        pend = mk()
                        u += 1
                pend()
                P.barrier()
                h2T = big[:, 0:65536].rearrange("p (k t) -> p k t", t=S)
                P.act(rstd2[:], ssq[:], AF.Sqrt, scale=1.0 / D, bias=eps_col[:, 0:1], reads=['ssq', 'eps_col'], writes=['ssq', 'rstd2'])
                P.op('dve', lambda e: e.reciprocal(out=rstd2[:], in_=rstd2[:]), reads=['rstd2'], writes=['rstd2'])
                xrow = [wbuf[i][:].rearrange("p k c -> p (k c)").bitcast(F32) for i in range(3)]
                for ct in range(32):
                    rs = ct % 3
                    P.dma('sp' if ct % 2 == 0 else 'pool', xrow[rs], x1T_d[ct * 128:(ct + 1) * 128, :],
                          ('a', 8 + rs) if ct % 2 == 0 else ('s', 4 + rs), writes=[('xrow', rs)])
                    P.tt('dve', xrow[rs], xrow[rs], rstd2[:], ALU.mult, reads=[('xrow', rs), 'rstd2'], writes=[('xrow', rs)])
                    P.act(h2T[:, ct, :], xrow[rs], AF.Identity, scale=gam_f[:, ct:ct + 1], bias=sh_f[:, ct:ct + 1],
                          reads=[('xrow', rs), 'gam_f', 'mod'], writes=[('h2T', ct)])
                P.barrier()

        if stop_after >= 6:
            h2T = big[:, 0:65536].rearrange("p (k t) -> p k t", t=S)
            with ExitStack() as ph:
                wu = [sb("wu%d" % i, [128, 32, 128], BF16, ph) for i in range(2)]
                wvb = [sb("wvb%d" % i, [128, 32, 128], BF16, ph) for i in range(2)]
                u_sb = [big[:, 65536 + i * 4104:65536 + i * 4104 + 4100].bitcast(F32) for i in range(2)]
                uc = [sb("uc%d" % i, [128, 1024], F32, ph) for i in range(2)]
                aT = [sb("aT%d" % i, [128, 1024], BF16, ph) for i in range(2)]
                conv_sb = sb("conv_sb", [128, 4, FC], F32, ph)
                P.dma('sp', conv_sb[:], conv_fm, ('a', 0), writes=['conv'])
                for i in range(2):
                    P.op('dve', (lambda i=i: lambda e: e.memset(u_sb[i][:, 0:2], 0.0))(), writes=[('u', i, 0)])
                wv = w_up.rearrange("(k p) c -> p k c", p=128)
                n = 0
                for j in range(FC):
                    slot = j % 2
                    P.dma('pool', wu[slot][:], wv[:, :, j * 128:(j + 1) * 128], ('w', slot), writes=[('wu', slot)])
                    P.dma('pool', wvb[slot][:], wv[:, :, DFF + j * 128:DFF + (j + 1) * 128], ('w', 2 + slot), writes=[('wv', slot)])
                    ub = u_sb[j % 2]
                    for th in range(2):
                        base = (n % 2) * 4
                        ns = n % 2
                        for t2 in range(2):
                            tok = th * 1024 + t2 * 512
                            for k in range(32):
                                P.mm(bank(base + t2), wu[slot][:, k, :], h2T[:, k, tok:tok + 512], start=(k == 0), stop=(k == 31),
                                     reads=[('wu', slot)], writes=[('ps', base + t2)], sig=(k == 31))
                        for t2 in range(2):
                            tok = th * 1024 + t2 * 512
                            for k in range(32):
                                P.mm(bank(base + 2 + t2), wvb[slot][:, k, :], h2T[:, k, tok:tok + 512], start=(k == 0), stop=(k == 31),
                                     reads=[('wv', slot)], writes=[('ps', base + 2 + t2)], sig=(k == 31))
                        o0 = th * 1024
                        P.cp('act', ub[:, 2 + o0:2 + o0 + 1024], PS[:, base * 512:(base + 2) * 512],
                             reads=[('ps', base), ('ps', base + 1)], writes=[('u', j % 2, 1 + th)])
                        rd = [('u', j % 2, 0), ('u', j % 2, 1), ('u', j % 2, 2), 'conv']
                        P.ts('dve', uc[ns][:], ub[:, o0:o0 + 1024], conv_sb[:, 0, j:j + 1], conv_sb[:, 3, j:j + 1], ALU.mult, ALU.add,
                             reads=rd, writes=[('uc', ns)])
                        P.stt('dve', uc[ns][:], ub[:, 1 + o0:1 + o0 + 1024], conv_sb[:, 1, j:j + 1], uc[ns][:], ALU.mult, ALU.add,
                              reads=rd + [('uc', ns)], writes=[('uc', ns)])
                        P.stt('dve', uc[ns][:], ub[:, 2 + o0:2 + o0 + 1024], conv_sb[:, 2, j:j + 1], uc[ns][:], ALU.mult, ALU.add,
                              reads=rd + [('uc', ns)], writes=[('uc', ns)])
                        P.act(uc[ns][:], uc[ns][:], AF.Silu, reads=[('uc', ns)], writes=[('uc', ns)])
                        P.tt('dve', aT[ns][:], uc[ns][:], PS[:, (base + 2) * 512:(base + 4) * 512], ALU.mult,
                             reads=[('uc', ns), ('ps', base + 2), ('ps', base + 3)], writes=[('aT', ns)])
                        P.dma('sp', actT_d[j * 128:(j + 1) * 128, o0:o0 + 1024], aT[ns][:], ('s', ns), reads=[('aT', ns)], writes=['actT'])
                        n += 1
                P.barrier()

        if stop_after >= 7:
            aTv = big[:, 0:44032].rearrange("p (k t) -> p k t", t=512)
            x2T = big[:, 45056:77824].bitcast(F32).rearrange("p (k t) -> p k t", t=512)
            with ExitStack() as ph:
                wd = [sb("wd%d" % i, [128, 22, 128], BF16, ph) for i in range(4)]
                xt_sb = [sb("xt7_%d" % i, [128, 512], F32, ph) for i in range(2)]
                sqb = [sb("sqb7_%d" % i, [128, 512], BF16, ph) for i in range(2)]
                ssq = sb("ssq7", [128, 512], F32, ph)
                rstd3 = sb("rstd3", [128, 512], F32, ph)
                yT = [sb("yT%d" % i, [128, 512], F32, ph) for i in range(2)]
                ostage = [sb("ostage%d" % i, [128, 512], F32, ph) for i in range(4)]
                wv = w_down.rearrange("(k p) c -> p k c", p=128)
                aT_dv = actT_d.rearrange("(k p) t -> p k t", p=128)
                gfin = gvec_sb[:, 2, :]
                kgs = [(0, 22), (22, 22), (44, 21), (65, 21)]
                wn = 0
                osn = 0
                def load_aT(tt):
                    for q, (k0, nk) in enumerate(kgs):
                        P.dma('sp', aTv[:, k0:k0 + nk, :], aT_dv[:, k0:k0 + nk, tt * 512:(tt + 1) * 512], ('a', q), writes=[('aTv', q)])
                load_aT(0)
                for tt in range(4):
                    P.op('dve', lambda e: e.memset(ssq[:], 0.0), writes=['ssq7'])
                    pend = None
                    for ct in range(32):
                        bk = ct % 4
                        sbk = 4 + ct % 4
                        us = ct % 2
                        P.dma('sp', xt_sb[us][:], x1T_d[ct * 128:(ct + 1) * 128, tt * 512:(tt + 1) * 512], ('a', 8 + us), writes=[('xt', us)])
                        for q, (k0, nk) in enumerate(kgs):
                            slot = wn % 4
                            wn += 1
                            P.dma('pool', wd[slot][:, 0:nk, :], wdb_d[ct].rearrange("p (k c) -> p k c", c=128)[:, k0:k0 + nk, :], ('w', slot), writes=[('wd', slot)])
                            for kk in range(nk):
                                P.mm(bank(bk), wd[slot][:, kk, :], aTv[:, k0 + kk, :], start=(q == 0 and kk == 0), stop=(q == 3 and kk == nk - 1),
                                     reads=[('wd', slot), ('aTv', q)], writes=[('ps', bk)], sig=(kk == nk - 1))
                        if pend is not None:
                            pend()
                        P.stt('dve', x2T[:, ct, :], bank(bk), gt_f[:, ct:ct + 1], xt_sb[us][:], ALU.mult, ALU.add,
                              reads=[('ps', bk), ('xt', us), 'mod'], writes=[('x2T', ct)])
                        P.act(sqb[us][:], x2T[:, ct, :], AF.Square, reads=[('x2T', ct)], writes=[('sqb', us)])

                        def mk(us=us, sbk=sbk):
                            def f():
                                P.mm(bank(sbk), ones_b[:], sqb[us][:], reads=[('sqb', us), 'ones_b'], writes=[('ps', sbk)])
                                P.tt('dve', ssq[:], bank(sbk), ssq[:], ALU.add, reads=[('ps', sbk), 'ssq7'], writes=['ssq7'])
                            return f
                        pend = mk()
                    pend()
                    if tt + 1 < 4:
                        load_aT(tt + 1)
                    P.act(rstd3[:], ssq[:], AF.Sqrt, scale=1.0 / D, bias=eps_col[:, 0:1], reads=['ssq7', 'eps_col'], writes=['rstd3'])
                    P.op('dve', lambda e: e.reciprocal(out=rstd3[:], in_=rstd3[:]), reads=['rstd3'], writes=['rstd3'])
                    for cg in range(8):
                        for c4 in range(4):
                            ct = cg * 4 + c4
                            ys = ct % 2
                            P.tt('dve', yT[ys][:], x2T[:, ct, :], rstd3[:], ALU.mult, reads=[('x2T', ct), 'rstd3'], writes=[('yT', ys)])
                            P.act(yT[ys][:], yT[ys][:], AF.Copy, scale=gfin[:, ct:ct + 1], reads=[('yT', ys), 'gvec'], writes=[('yT', ys)])
                            for tb4 in range(4):
                                bkk = (cg % 2) * 4 + tb4
                                P.tr(bank(bkk)[:, c4 * 128:(c4 + 1) * 128], yT[ys][:, tb4 * 128:(tb4 + 1) * 128], ident_f[:],
                                     reads=[('yT', ys), 'ident_f'], writes=[('ps', bkk)], sig=(tb4 == 3))
                        for tb4 in range(4):
                            bkk = (cg % 2) * 4 + tb4
                            so = osn % 4
                            osn += 1
                            P.cp('act' if tb4 % 2 == 0 else 'dve', ostage[so][:], bank(bkk), reads=[('ps', bkk)], writes=[('ostage', so)])
                            t0 = tt * 512 + tb4 * 128
                            P.dma('sp', out[t0:t0 + 128, cg * 512:(cg + 1) * 512], ostage[so][:], ('s', so), reads=[('ostage', so)])
                P.barrier()


        if dbg:
            P.barrier()
            D_('sp', dbg_out["d_oT"], oT_d, None, ('s', 0))
            D_('sp', dbg_out["d_x1T"], x1T_d, None, ('s', 1))
        P.barrier()
        P.emit()
    return nc


def prep_inputs(inputs):
    g = lambda n: np.asarray(inputs[n], np.float32)
    shared = {}
    shared["w_ada"] = np.ascontiguousarray(g("w_ada")[0].reshape(32, 128, 192, 128).transpose(2, 1, 0, 3)).reshape(192, 128, D)
    shared["b_ada_fm"] = fm(g("b_ada")[0])
    shared["gvec"] = np.ascontiguousarray(np.stack([fm(g("g_norm_mix")[0]), fm(g("g_norm_ffn")[0]), fm(g("g_norm_final"))], axis=1))
    shared["w_in"] = np.ascontiguousarray(g("w_in")[0])
    mu = g("mu_shift")[0]
    mu_fm = np.zeros((128, 52), np.float32)
    mu_fm[:, 0:48] = fm(mu[0:6144])
    mu_fm[0:96, 48] = mu[6144:6240]
    mu_fm[0:96, 49] = mu[6240:6336]
    mu_fm[:, 50] = mu[6336:6464]
    mu_fm[:, 51] = mu[6464:6592]
    shared["mu_fm"] = mu_fm
    shared["rwp"] = np.ascontiguousarray(np.stack([fm(g(n)[0]) for n in ("w0", "a0", "k_k", "k_a", "r_k", "ln_x_w", "ln_x_b")], axis=1))
    shared["w_dec"] = np.ascontiguousarray(g("w_decay_up")[0])
    shared["w_aaa"] = np.ascontiguousarray(g("w_aaa_up")[0])
    shared["w_gate"] = np.ascontiguousarray(g("w_gate_up")[0])
    shared["gsb_fm"] = np.ascontiguousarray(g("g_sb_out")[0].T)
    shared["w_out"] = np.ascontiguousarray(g("w_out")[0])
    shared["w_up"] = np.ascontiguousarray(g("w_up")[0])
    cw = g("conv_w")[0]
    shared["conv_fm"] = np.ascontiguousarray(np.stack([fm(cw[0]), fm(cw[1]), fm(cw[2]), fm(g("conv_b")[0])], axis=1))
    shared["w_down"] = np.ascontiguousarray(g("w_down")[0])
    for k, v in host_consts().items():
        shared["c_" + k] = v
    x = g("x")
    c = g("c")
    per = []
    for b in range(x.shape[0]):
        m = dict(shared)
        m["x"] = np.ascontiguousarray(x[b])
        m["c_fm"] = fm(c[b])
        per.append(m)
    return per


_NC = None


def kernel(**inputs):
    global _NC
    per = prep_inputs(inputs)
    if _NC is None:
        _NC = build()
    res = run_bass_kernel_spmd(_NC, per, core_ids=list(range(8)))
    return np.stack([np.asarray(r["out"], np.float32) for r in res.results], axis=0)
```

```python
import numpy as np
from contextlib import ExitStack
import concourse.bass as bass
import concourse.mybir as mybir
from concourse.bass_utils import run_bass_kernel_spmd

F32 = mybir.dt.float32
BF16 = mybir.dt.bfloat16
F32R = mybir.dt.float32r
AF = mybir.ActivationFunctionType
ALU = mybir.AluOpType

D = 4096
S = 2048
KC = 32
INC = 12736
DFF = 11008
FC = 86
NORM_EPS = 1e-6
GN_EPS = 64e-5
BIGN = 77824


class Prog:
    ENG = ('pe', 'act', 'dve', 'pool', 'sp')

    def __init__(self, nc, stack):
        self.nc = nc
        self.stack = stack
        self.ops = {e: [] for e in self.ENG}
        self.sems = {}
        self.semcnt = {}
        self.waited = {e: {} for e in self.ENG}
        self.buf = {}
        for e in self.ENG:
            self._sem(('e', e))

    def _sem(self, key):
        if key not in self.sems:
            name = "s_" + "_".join(str(k) for k in key)
            self.sems[key] = self.stack.enter_context(self.nc.semaphore(name))
            self.semcnt[key] = 0
        return self.sems[key]

    def op(self, eng, fn, reads=(), writes=(), dma=None, sig=True, extra=(), same=True):
        needs = {}

        def need(ev):
            if ev is None:
                return
            k, v = ev
            if needs.get(k, 0) < v:
                needs[k] = v
        for k in reads:
            b = self.buf.get(k)
            if b:
                need(b['w'])
        for k in writes:
            b = self.buf.get(k)
            if b:
                need(b['w'])
                for ev in b['r'].items():
                    need(ev)
        for ev in extra:
            need(ev)
        waits = []
        for k, v in needs.items():
            if k == ('e', eng) and (eng == 'pe' or not same):
                continue
            if self.waited[eng].get(k, 0) >= v:
                continue
            self.waited[eng][k] = v
            waits.append((self.sems[k], v))
        if dma is not None:
            self._sem(dma)
            self.semcnt[dma] += 16
            ev = (dma, self.semcnt[dma])
            inc = (self.sems[dma], 16)
        else:
            k = ('e', eng)
            if sig:
                self.semcnt[k] += 1
                ev = (k, self.semcnt[k])
                inc = (self.sems[k], 1)
            else:
                ev = (k, self.semcnt[k] + 1)
                inc = None
        self.ops[eng].append((waits, fn, inc))
        for k in reads:
            b = self.buf.setdefault(k, {'w': None, 'r': {}})
            if b['r'].get(ev[0], 0) < ev[1]:
                b['r'][ev[0]] = ev[1]
        for k in writes:
            self.buf[k] = {'w': ev, 'r': {}}
        return ev

    def mm(self, out, lhsT, rhs, start=True, stop=True, **kw):
        return self.op('pe', lambda e: e.matmul(out, lhsT, rhs, start=start, stop=stop), **kw)

    def tr(self, out, in_, ident, **kw):
        return self.op('pe', lambda e: e.transpose(out, in_, ident), **kw)

    def act(self, out, in_, func, scale=1.0, bias=None, **kw):
        if bias is None:
            return self.op('act', lambda e: e.activation(out=out, in_=in_, func=func, scale=scale), **kw)
        return self.op('act', lambda e: e.activation(out=out, in_=in_, func=func, scale=scale, bias=bias), **kw)

    def cp(self, eng, out, in_, **kw):
        if eng == 'act':
            return self.op('act', lambda e: e.copy(out=out, in_=in_), **kw)
        return self.op(eng, lambda e: e.tensor_copy(out=out, in_=in_), **kw)

    def tt(self, eng, out, in0, in1, op, **kw):
        return self.op(eng, lambda e: e.tensor_tensor(out=out, in0=in0, in1=in1, op=op), **kw)

    def ts(self, eng, out, in0, s1, s2, op0, op1=None, **kw):
        if op1 is None:
            return self.op(eng, lambda e: e.tensor_scalar(out=out, in0=in0, scalar1=s1, scalar2=None, op0=op0), **kw)
        return self.op(eng, lambda e: e.tensor_scalar(out=out, in0=in0, scalar1=s1, scalar2=s2, op0=op0, op1=op1), **kw)

    def stt(self, eng, out, in0, scalar, in1, op0, op1, **kw):
        return self.op(eng, lambda e: e.scalar_tensor_tensor(out=out, in0=in0, scalar=scalar, in1=in1, op0=op0, op1=op1), **kw)

    def dma(self, eng, out, in_, sem, **kw):
        return self.op(eng, lambda e: e.dma_start(out=out, in_=in_), dma=sem, **kw)

    def barrier(self):
        evs = [(k, c) for k, c in self.semcnt.items() if c > 0]
        for e in self.ENG:
            self.op(e, None, extra=evs, sig=False)
        self.buf = {}

    def emit(self):
        nc = self.nc
        with nc.Block() as block:
            def run(e, lst):
                for waits, fn, inc in lst:
                    for s, v in waits:
                        e.wait_ge(s, v)
                    if fn is None:
                        continue
                    ins = fn(e)
                    if inc is not None:
                        ins.then_inc(inc[0], inc[1])

            @block.tensor
            def _(e):
                run(e, self.ops['pe'])

            @block.scalar
            def _(e):
                run(e, self.ops['act'])

            @block.vector
            def _(e):
                run(e, self.ops['dve'])

            @block.gpsimd
            def _(e):
                run(e, self.ops['pool'])

            @block.sync
            def _(e):
                run(e, self.ops['sp'])


def host_consts():
    c = {}
    c["ident"] = np.eye(128, dtype=np.float32)
    j = np.arange(128)[:, None]
    s = np.arange(128)[None, :]
    c["tri_incl"] = (j >= s).astype(np.float32)
    c["tri_c"] = (j < s).astype(np.float32)
    t = np.arange(512)[None, None, :]
    r = np.arange(4)[None, :, None]
    c["amask"] = (t > r * 128 + np.arange(128)[:, None, None]).astype(np.float32)
    c["blockones"] = ((j // 64) == (s // 64)).astype(np.float32)
    rm = np.ones((128, S), np.float32)
    rm[:, ::64] = 0.0
    c["resetmask"] = rm
    s64 = np.arange(64)[:, None]
    t64 = np.arange(64)[None, :]
    strict = (s64 < t64).astype(np.float32)
    incl = (s64 <= t64).astype(np.float32)
    m_si = np.concatenate([strict, incl], axis=1)
    m_lo = (s64 > t64).astype(np.float32)
    c["msk_si4"] = np.tile(m_si, (2, 4))
    c["msk_lo8"] = np.tile(m_lo, (2, 8))
    c["identx"] = np.tile(np.eye(64, dtype=np.float32), (2, 1))
    return c


def fm(v, p=128):
    v = np.asarray(v, np.float32)
    return np.ascontiguousarray(v.reshape(-1, p).T)


def build(stop_after=99, dbg=False, skip_rwkv=False):
    nc = bass.Bass("TRN2", target_bir_lowering=False)

    def din(name, shape, dt=F32):
        return nc.dram_tensor(name, list(shape), dt, kind="ExternalInput").ap()

    def dscr(name, shape, dt):
        return nc.dram_tensor(name, list(shape), dt, kind="Internal").ap()

    x = din("x", [S, D])
    c_fm = din("c_fm", [128, 32])
    w_ada = din("w_ada", [192, 128, D])
    b_ada_fm = din("b_ada_fm", [128, 192])
    gvec = din("gvec", [128, 3, 32])
    w_in = din("w_in", [D, INC])
    mu_fm = din("mu_fm", [128, 52])
    rwp = din("rwp", [128, 7, 16])
    w_dec = din("w_dec", [96, 2048])
    w_aaa = din("w_aaa", [96, 2048])
    w_gate = din("w_gate", [256, 2048])
    gsb_fm = din("gsb_fm", [128, 16])
    w_out = din("w_out", [D, D])
    w_up = din("w_up", [D, 2 * DFF])
    conv_fm = din("conv_fm", [128, 4, FC])
    w_down = din("w_down", [DFF, D])
    cst = {k: din("c_" + k, v.shape) for k, v in host_consts().items()}
    out = nc.dram_tensor("out", [S, D], F32, kind="ExternalOutput").ap()

    xT_d = dscr("xT_d", [D, S], F32)
    projT_d = dscr("projT_d", [12800, S], BF16)
    oT_d = dscr("oT_d", [D, S], BF16)
    x1T_d = dscr("x1T_d", [D, S], F32)
    actT_d = dscr("actT_d", [DFF, S], BF16)
    wdb_d = dscr("wdb_d", [32, 128, FC * 128], BF16)
    dbg_out = {}
    if dbg:
        dbg_out["d_projT"] = nc.dram_tensor("d_projT", [12800, S], BF16, kind="ExternalOutput").ap()
        dbg_out["d_oT"] = nc.dram_tensor("d_oT", [D, S], BF16, kind="ExternalOutput").ap()
        dbg_out["d_mod"] = nc.dram_tensor("d_mod", [128, 192], F32, kind="ExternalOutput").ap()
        dbg_out["d_x1T"] = nc.dram_tensor("d_x1T", [D, S], F32, kind="ExternalOutput").ap()

    with ExitStack() as st:
        P = Prog(nc, st)

        def sb(name, shape, dt=F32, stack=st):
            return stack.enter_context(nc.sbuf_tensor(name, list(shape), dt))

        PS = st.enter_context(nc.psum_tensor("PS", [128, 4096], F32))

        def bank(b):
            return PS[:, b * 512:(b + 1) * 512]

        big = sb("big", [128, BIGN], BF16)
        ident_f = sb("ident_f", [128, 128])
        ident_b = sb("ident_b", [128, 128], BF16)
        ones_b = sb("ones_b", [128, 128], BF16)
        tri_incl = sb("tri_incl", [128, 128])
        tri_c = sb("tri_c", [128, 128])
        amask = sb("amask", [128, 4, 512], BF16)
        blockones = sb("blockones", [128, 128])
        one11 = sb("one11", [1, 1])
        eps_col = sb("eps_col", [128, 1])
        gn_eps_col = sb("gn_eps_col", [128, 1])
        one_col = sb("one_col", [128, 1])
        mod_fm = sb("mod_fm", [128, 192])
        bada_sb = sb("bada_sb", [128, 192])
        gvec_sb = sb("gvec_sb", [128, 3, 32])
        gam_m = sb("gam_m", [128, 32])
        gam_f = sb("gam_f", [128, 32])
        gsb_sb = sb("gsb_sb", [128, 16])
        c_sb = sb("c_sb", [128, 32])
        sc_bf = sb("sc_bf", [128, 32], BF16)
        sh_m = mod_fm[:, 0:32]
        gt_m = mod_fm[:, 64:96]
        sh_f = mod_fm[:, 96:128]
        gt_f = mod_fm[:, 160:192]

        def D_(eng, out_, in_, key, dsem, reads=(), writes=()):
            return P.op(eng, lambda e: e.dma_start(out=out_, in_=in_), reads=reads, writes=writes, dma=dsem)

        D_('sp', ident_f[:], cst["ident"], None, ('a', 0), writes=['ident_f'])
        D_('pool', ident_b[:], cst["ident"], None, ('w', 0), writes=['ident_b'])
        D_('sp', tri_incl[:], cst["tri_incl"], None, ('a', 1), writes=['tri_incl'])
        D_('sp', tri_c[:], cst["tri_c"], None, ('a', 2), writes=['tri_c'])
        D_('pool', amask[:], cst["amask"], None, ('w', 1), writes=['amask'])
        D_('sp', blockones[:], cst["blockones"], None, ('a', 3), writes=['blockones'])
        D_('sp', bada_sb[:], b_ada_fm, None, ('a', 4), writes=['bada'])
        D_('sp', gvec_sb[:], gvec, None, ('a', 5), writes=['gvec'])
        D_('sp', gsb_sb[:], gsb_fm, None, ('s', 0), writes=['gsb'])
        D_('sp', c_sb[:], c_fm, None, ('s', 1), writes=['c_sb'])
        P.op('dve', lambda e: e.memset(ones_b[:], 1.0), writes=['ones_b'])
        P.op('dve', lambda e: e.memset(one11[:], 1.0), writes=['one11'])
        P.op('dve', lambda e: e.memset(eps_col[:], NORM_EPS), writes=['eps_col'])
        P.op('dve', lambda e: e.memset(gn_eps_col[:], GN_EPS), writes=['gn_eps_col'])
        P.op('dve', lambda e: e.memset(one_col[:], 1.0), writes=['one_col'])

        P.act(sc_bf[:], c_sb[:], AF.Silu, reads=['c_sb'], writes=['sc_bf'])
        hT = big[:, 0:65536].rearrange("p (k t) -> p k t", t=S)

        def ada_group(g, abuf):
            slot = g % 2
            P.dma('pool', abuf[slot].rearrange("p k c -> p (k c)"), w_ada[g], ('w', 4 + slot), writes=[('adaw', slot)])
            for k in range(32):
                P.mm(bank(7)[:, g:g + 1], abuf[slot][:, k, :], sc_bf[:, k:k + 1], start=(k == 0), stop=(k == 31),
                     reads=[('adaw', slot), 'sc_bf'], writes=['modps'], sig=(k == 31))

        def ada_flush():
            pass

        with ExitStack() as ph:
            xtm = sb("xtm", [128, D], F32, ph)
            abuf = [sb("abuf%d" % i, [128, 32, 128], BF16, ph)[:] for i in range(2)]
            xTb = big[:, 65536:73728].bitcast(F32)
            sq = [sb("sq%d" % i, [128, 512], BF16, ph) for i in range(2)]
            rt = sb("rt", [128, 128], F32, ph)
            rstd = sb("rstd", [128, 128], F32, ph)
            xT_dv = xT_d.rearrange("(k p) t -> p k t", p=128)
            import os
            VAR = os.environ.get("KVAR", "")
            if 'A' in VAR:
                for g in range(64):
                    ada_group(g, abuf)
                ada_flush()
            for tb in range(16):
                if 'A' not in VAR:
                    for g in range(4 * tb, 4 * tb + 4):
                        ada_group(g, abuf)
                P.dma('sp', xtm[:], x[tb * 128:(tb + 1) * 128, :], ('a', 0), writes=['xtm'])
                sbk = 4 + tb % 2
                for q in range(8):
                    bk = q % 4
                    for j in range(4):
                        k = q * 4 + j
                        P.tr(bank(bk)[:, j * 128:(j + 1) * 128], xtm[:, k * 128:(k + 1) * 128], ident_f[:],
                             reads=['xtm', 'ident_f'], writes=[('ps', bk)], sig=(j == 3))
                    P.cp('act', xTb[:, q * 512:(q + 1) * 512], bank(bk), reads=[('ps', bk)], writes=[('xTb', q)])
                    P.tt('dve', sq[q % 2][:], xTb[:, q * 512:(q + 1) * 512], xTb[:, q * 512:(q + 1) * 512], ALU.mult,
                         reads=[('xTb', q)], writes=[('sq', q % 2)])
                    for j in range(4):
                        P.mm(bank(sbk)[:, 0:128], ones_b[:], sq[q % 2][:, j * 128:(j + 1) * 128],
                             start=(q == 0 and j == 0), stop=(q == 7 and j == 3),
                             reads=[('sq', q % 2), 'ones_b'], writes=[('ps', sbk)], sig=(j == 3))
                P.act(rt[:], bank(sbk)[:, 0:128], AF.Sqrt, scale=1.0 / D, bias=eps_col[:, 0:1], reads=[('ps', sbk), 'eps_col'], writes=['rt'])
                P.op('dve', lambda e: e.reciprocal(out=rstd[:], in_=rt[:]), reads=['rt'], writes=['rstd'])
                P.tt('dve', hT[:, :, tb * 128:(tb + 1) * 128], xTb.rearrange("p (k t) -> p k t", t=128),
                     rstd[:].rearrange("p (o t) -> p o t", o=1).to_broadcast([128, 32, 128]), ALU.mult,
                     reads=[('xTb', q) for q in range(8)] + ['rstd'], writes=[('hT', tb)])
                P.dma('sp', xT_dv[:, :, tb * 128:(tb + 1) * 128], xTb.rearrange("p (k t) -> p k t", t=128), ('s', 0),
                      reads=[('xTb', q) for q in range(8)])
            ada_flush()
            P.tt('dve', mod_fm[:, 0:64], bank(7)[:, 0:64], bada_sb[:, 0:64], ALU.add, reads=['modps', 'bada'], writes=['mod'])
            P.stt('dve', gam_m[:], mod_fm[:, 32:64], 1.0, gvec_sb[:, 0, :], ALU.add, ALU.mult, reads=['mod', 'gvec'], writes=['gam_m'])
            for k in range(32):
                P.act(hT[:, k, :], hT[:, k, :], AF.Identity, scale=gam_m[:, k:k + 1], bias=sh_m[:, k:k + 1],
                      reads=[('hT', tb) for tb in range(16)] + ['gam_m', 'mod'], writes=[('hTk', k)], same=False)
            P.barrier()

        if stop_after >= 2:
            with ExitStack() as ph:
                wbuf = [sb("wbuf%d" % i, [128, 32, 128], BF16, ph) for i in range(3)]
                stage = [sb("stage%d" % i, [128, S], BF16, ph) for i in range(2)]
                abuf = [big[:, 65536 + i * 4096:65536 + (i + 1) * 4096].rearrange("p (k c) -> p k c", c=128) for i in range(2)]
                wv = w_in.rearrange("(k p) c -> p k c", p=128)
                ag = 64
                u = 0
                for ct in range(100):
                    cw = 128 if ct < 99 else 64
                    slot = ct % 3
                    P.dma('pool', wbuf[slot][:, :, 0:cw], wv[:, :, ct * 128:ct * 128 + cw], ('w', slot), writes=[('wb', slot)])
                    for tt in range(4):
                        bk = u % 6
                        u += 1
                        for k in range(32):
                            P.mm(bank(bk)[0:cw, :], wbuf[slot][:, k, 0:cw], hT[:, k, tt * 512:(tt + 1) * 512], start=(k == 0), stop=(k == 31),
                                 reads=[('wb', slot)], writes=[('ps', bk)], sig=(k == 31))
                        P.cp('act' if tt % 2 == 0 else 'dve', stage[ct % 2][0:cw, tt * 512:(tt + 1) * 512], bank(bk)[0:cw, :],
                             reads=[('ps', bk)], writes=[('stg', ct % 2, tt)])
                    P.dma('sp', projT_d[ct * 128:ct * 128 + cw, :], stage[ct % 2][0:cw, :], ('s', ct % 2),
                          reads=[('stg', ct % 2, tt) for tt in range(4)], writes=['projT'])
                    for _ in range(2 if ct < 28 else 1):
                        ada_group(ag, abuf)
                        ag += 1
                assert ag == 192
                ada_flush()
                P.tt('dve', mod_fm[:, 64:192], bank(7)[:, 64:192], bada_sb[:, 64:192], ALU.add, reads=['modps', 'bada'], writes=['mod'])
                P.stt('dve', gam_f[:], mod_fm[:, 128:160], 1.0, gvec_sb[:, 1, :], ALU.add, ALU.mult, reads=['mod', 'gvec'], writes=['gam_f'])
                if dbg:
                    P.dma('sp', dbg_out["d_mod"], mod_fm[:], ('s', 2), reads=['mod'])
                P.barrier()
                if dbg:
                    P.dma('sp', dbg_out["d_projT"][0:12736, :], projT_d[0:12736, :], ('s', 2), reads=['projT'])

        if stop_after >= 3:
            with ExitStack() as ph:
                NQ = 2048
                e_sb = [[sb("e_sb%d%d" % (i, j), [128, 512], F32, ph) for j in range(2)] for i in range(2)]
                L_sb = [[sb("L_sb%d%d" % (i, j), [128, 512], F32, ph) for j in range(2)] for i in range(2)]
                eg_sb = [[big[:, 32768 + (2 * i + j) * 1024:32768 + (2 * i + j + 1) * 1024].bitcast(F32) for j in range(2)] for i in range(2)]
                aw_sb = [[sb("aw_sb%d%d" % (i, j), [128, 512], BF16, ph) for j in range(2)] for i in range(2)]
                osb = [sb("osb%d" % i, [128, 512], F32, ph) for i in range(2)]
                osq = [big[:, 36864 + i * 512:36864 + (i + 1) * 512] for i in range(2)]
                ort = [sb("ort%d" % i, [128, 512], F32, ph) for i in range(2)]
                ostg = [sb("ostg%d" % i, [128, 512], BF16, ph) for i in range(2)]
                SCALE = float(1.0 / np.sqrt(128.0))
                wdv = w_down.rearrange("(k p) c -> p k c", p=128)

                def precast(first_extra):
                    pc_ev = []
                    for ct in range(32):
                        for (k0, nk) in ((0, 22), (22, 22), (44, 21), (65, 21)):
                            n = len(pc_ev)
                            ev = P.dma('pool', wdb_d[ct].rearrange("p (k c) -> p k c", c=128)[:, k0:k0 + nk, :],
                                       wdv[:, k0:k0 + nk, ct * 128:(ct + 1) * 128], ('w', 4 + n % 4),
                                       extra=([pc_ev[n - 4]] if n >= 4 else list(first_extra)))
                            pc_ev.append(ev)

                def hbuf(pp, i, j):
                    o = ((pp * 2 + i) * 3 + j) * NQ
                    return big[:, o:o + NQ]

                def vtm(i):
                    o = 24576 + i * NQ
                    return big[:, o:o + NQ].rearrange("p (b d) -> p b d", d=128)

                Ls = [[sb("Ls%d%d" % (i, j), [128, 512], F32, ph)[:] for j in range(2)] for i in range(2)]
                ones_f = sb("ones_r", [128, 128], F32R, ph)[:]
                tri_r = sb("tri_r", [128, 128], F32R, ph)[:]
                P.cp('dve', ones_f, one_col[:, 0:1].to_broadcast([128, 128]), reads=['one_col'], writes=['ones_f'])
                P.cp('dve', tri_r, tri_incl[:], reads=['tri_incl'], writes=['tri_r'])

                def stageA(pp, i, qt, idx):
                    kb = 4 * qt + 3 - idx
                    zb = 4 * i
                    sl = idx % 2
                    P.mm(bank(zb), hbuf(pp, i, 1)[:, kb * 128:(kb + 1) * 128], hbuf(pp, i, 0)[:, qt * 512:(qt + 1) * 512],
                         reads=[('qkv', pp, i, 0), ('qkv', pp, i, 1)], writes=[('ps', zb)])
                    P.act(e_sb[i][sl][:], bank(zb), AF.Exp, scale=SCALE, reads=[('ps', zb)], writes=[('e', i, sl)])
                    la, lk = lbuf(i, idx)
                    P.act(la.bitcast(F32R), e_sb[i][sl][:], AF.Ln, bias=one_col[:, 0:1], reads=[('e', i, sl), 'one_col'], writes=[lk])
                    if kb >= 4 * qt:
                        r = kb - 4 * qt
                        P.tt('dve', la.bitcast(F32R), la, amask[:, r, :], ALU.mult, reads=[lk, 'amask'], writes=[lk])
                        P.tt('dve', e_sb[i][sl][:], e_sb[i][sl][:], amask[:, r, :], ALU.mult, reads=[('e', i, sl), 'amask'], writes=[('e', i, sl)])

                def lbuf(i, idx):
                    if idx == 0:
                        return Ls[i][0][:], ('Ls', i, 0)
                    return L_sb[i][idx % 2][:], ('L', i, idx % 2)

                def lprev(i, idx):
                    return Ls[i][(idx - 1) % 2][:], ('Ls', i, (idx - 1) % 2)

                def cum(i, idx):
                    cb = 4 * i + 1 + idx % 2
                    la, lk = lbuf(i, idx)
                    if idx == 0:
                        P.mm(bank(cb), tri_r, la.bitcast(F32R), start=True, stop=True, reads=[lk, 'tri_r'], writes=[('ps', cb)])
                    else:
                        pa, pk = lprev(i, idx)
                        P.mm(bank(cb), ones_f, pa.bitcast(F32R), start=True, stop=False, reads=[pk, 'ones_f'], writes=[('ps', cb)], sig=False)
                        P.mm(bank(cb), tri_r, la.bitcast(F32R), start=False, stop=True, reads=[lk, 'tri_r', pk], writes=[('ps', cb)])

                def lsum(i, idx):
                    pa, pk = lprev(i, idx)
                    la, lk = lbuf(i, idx)
                    P.tt('dve', Ls[i][idx % 2].bitcast(F32R), pa, la, ALU.add, reads=[pk, lk], writes=[('Ls', i, idx % 2)])

                def egf(i, idx):
                    cb = 4 * i + 1 + idx % 2
                    sl = idx % 2
                    P.act(eg_sb[i][sl][:], bank(cb), AF.Exp, scale=-1.0, reads=[('ps', cb)], writes=[('eg', i, sl)])

                def awf(i, idx):
                    sl = idx % 2
                    P.tt('dve', aw_sb[i][sl][:], e_sb[i][sl][:], eg_sb[i][sl][:], ALU.mult, reads=[('e', i, sl), ('eg', i, sl)], writes=[('aw', i, sl)])

                def avf(i, qt, idx, last):
                    kb = 4 * qt + 3 - idx
                    ob = 4 * i + 3
                    sl = idx % 2
                    P.mm(bank(ob), vtm(i)[:, kb, :], aw_sb[i][sl][:], start=(idx == 0), stop=last,
                         reads=[('aw', i, sl), ('vtm', i, 0), ('vtm', i, 1)], writes=[('ps', ob)])

                def final1(i, h, qt):
                    ob = 4 * i + 3
                    nb_ = 4 * i + 2
                    P.cp('act', osb[i][:], bank(ob), reads=[('ps', ob)], writes=[('osb', i)])
                    P.tt('dve', osq[i][:], osb[i][:], osb[i][:], ALU.mult, reads=[('osb', i)], writes=[('osq', i)])
                    P.mm(bank(nb_), ones_b[:], osq[i][:], reads=[('osq', i), 'ones_b'], writes=[('ps', nb_)])

                def final2(i, h, qt):
                    nb_ = 4 * i + 2
                    P.act(ort[i][:], bank(nb_), AF.Ln, scale=1.0 / 128.0, bias=eps_col[:, 0:1], reads=[('ps', nb_), 'eps_col'], writes=[('ort', i)])
                    P.act(ort[i][:], ort[i][:], AF.Exp, scale=-0.5, reads=[('ort', i)], writes=[('ort', i)])
                    P.stt('dve', ostg[i][:], osb[i][:], gsb_sb[:, h:h + 1], ort[i][:], ALU.mult, ALU.mult,
                          reads=[('osb', i), ('ort', i), 'gsb'], writes=[('ostg', i)])
                    P.dma('sp', oT_d[h * 128:(h + 1) * 128, qt * 512:(qt + 1) * 512], ostg[i][:], ('s', i), reads=[('ostg', i)], writes=['oT'])

                def load_qkv(hp):
                    pp = hp % 2
                    evs = []
                    for i in range(2):
                        h = 2 * hp + i
                        for j in range(3):
                            evs.append(P.dma('sp', hbuf(pp, i, j), projT_d[j * 2048 + h * 128:j * 2048 + (h + 1) * 128, :],
                                             ('a', (pp * 2 + i) * 3 + j), writes=[('qkv', pp, i, j)]))
                    return evs

                pend = []
                precast(load_qkv(0))
                started = False
                for hp in range(8):
                    pp = hp % 2
                    heads = (2 * hp, 2 * hp + 1)
                    for i in range(2):
                        for half in range(2):
                            bk = 4 * i + half
                            bkb = bank(bk).bitcast(BF16)
                            for t8 in range(8):
                                tb = half * 8 + t8
                                P.tr(bkb[:, t8 * 128:(t8 + 1) * 128], hbuf(pp, i, 2)[:, tb * 128:(tb + 1) * 128], ident_b[:],
                                     reads=[('qkv', pp, i, 2), 'ident_b'], writes=[('ps', bk)], sig=(t8 == 7))
                            dst = big[:, 24576 + i * NQ + half * 1024:24576 + i * NQ + (half + 1) * 1024]
                            P.cp('dve' if half else 'act', dst, bkb[:, 0:1024], reads=[('ps', bk)], writes=[('vtm', i, half)])
                    for qt in range(4):
                        n_it = 4 * qt + 4
                        if qt == 3 and hp + 1 < 8:
                            load_qkv(hp + 1)
                        if not started:
                            for i in range(2):
                                stageA(pp, i, qt, 0)
                        started = False
                        for idx in range(n_it):
                            if idx + 1 < n_it:
                                for i in range(2):
                                    stageA(pp, i, qt, idx + 1)
                            if idx == 0:
                                for f in pend:
                                    f()
                                pend = []
                            for i in range(2):
                                cum(i, idx)
                            if idx == n_it - 1 and qt + 1 < 4:
                                for i in range(2):
                                    stageA(pp, i, qt + 1, 0)
                                started = True
                            if idx >= 1:
                                for i in range(2):
                                    avf(i, qt, idx - 1, False)
                            for i in range(2):
                                egf(i, idx)
                            if 1 <= idx < n_it - 1:
                                for i in range(2):
                                    lsum(i, idx)
                            for i in range(2):
                                awf(i, idx)
                        for i in range(2):
                            avf(i, qt, n_it - 1, True)
                        for i in range(2):
                            final1(i, heads[i], qt)
                        for i in range(2):
                            pend.append((lambda i=i, h=heads[i], qt=qt: final2(i, h, qt)))
                for f in pend:
                    f()
                P.barrier()

        if stop_after >= 4 and not skip_rwkv:
            with ExitStack() as ph:
                NEG_E = -float(np.exp(-0.5))
                _off = [0]

                def carve(nbf16):
                    o = _off[0]
                    _off[0] += (nbf16 + 1) // 2 * 2
                    assert _off[0] <= BIGN, _off[0]
                    return big[:, o:o + nbf16]

                def cf32(n):
                    return carve(2 * n).bitcast(F32)
                lds = [carve(2052)[:, 0:2049] for _ in range(3)]
                T1, T2, T3, T4, T5 = [cf32(S) for _ in range(5)]
                LW = cf32(S)
                LG = cf32(S)
                vb = carve(S)
                AR = carve(2 * S).rearrange("p (c t) -> p c t", t=128)
                BBAR = carve(S)
                KBAR = carve(S)
                oBT = _off[0]
                BT = carve(S)
                KT = carve(S)
                TMPK = big[:, oBT:oBT + 2 * S].bitcast(F32)
                Vtm = carve(S)
                BtT = carve(S)
                KtT = carve(S)
                AB = carve(2 * S).rearrange("p (c t) -> p c t", t=128)
                AK = carve(2 * S).rearrange("p (c t) -> p c t", t=128)
                oPa = _off[0]
                Pa, PTa = carve(S), carve(S)
                PA2 = big[:, oPa:oPa + 2 * S].bitcast(F32)
                oPb = _off[0]
                Pb, PTb, Tm = carve(S), carve(S), carve(S)
                BON = big[:, oPb:oPb + 2 * S].bitcast(F32)
                Ysb = LW
                ostg4 = carve(S)
                txw = sb("txw", [128, S], BF16, ph)
                xa_m = sb("xa_m", [128, S], BF16, ph)
                sxg = sb("sxg", [128, 2, S], BF16, ph)
                w_dec_b = sb("w_dec_b", [128, S], BF16, ph)
                w_aaa_b = sb("w_aaa_b", [128, S], BF16, ph)
                w_gate_b = sb("w_gate_b", [128, 2, S], BF16, ph)
                rmask = sb("rmask", [128, S], BF16, ph)
                msk_si4 = sb("msk_si4", [128, 512], BF16, ph)
                msk_lo8 = sb("msk_lo8", [128, 512], BF16, ph)
                identx = sb("identx", [128, 64], BF16, ph)
                rwp_sb = sb("rwp_sb", [128, 7, 16], F32, ph)
                mu_sb = sb("mu_sb", [128, 52], F32, ph)
                GC = sb("GC", [128, 32], F32, ph)
                Zf = sb("Zf", [128, 64], F32, ph)
                Zb = sb("Zb", [128, 64], BF16, ph)
                Xs = sb("Xs", [128, 64], BF16, ph)
                Us = sb("Us", [128, 64], BF16, ph)
                tiny = sb("tiny", [128, 1], F32, ph)

                def v3(ap, t=64):
                    return ap.rearrange("p (c t) -> p c t", t=t)
                P.dma('pool', w_dec_b[0:96, :], w_dec, ('w', 0), writes=['w_dec_b'])
                P.dma('pool', w_aaa_b[0:96, :], w_aaa, ('w', 1), writes=['w_aaa_b'])
                P.dma('pool', w_gate_b[:], w_gate.rearrange("(j p) c -> p j c", p=128), ('w', 2), writes=['w_gate_b'])
                P.dma('pool', rmask[:], cst["resetmask"], ('w', 3), writes=['rmask'])
                P.dma('pool', msk_si4[:], cst["msk_si4"], ('s', 1), writes=['msk_si4'])
                P.dma('pool', msk_lo8[:], cst["msk_lo8"], ('s', 2), writes=['msk_lo8'])
                P.dma('pool', identx[:], cst["identx"], ('s', 3), writes=['identx'])
                P.dma('sp', rwp_sb[:], rwp, ('a', 0), writes=['rwp'])
                P.dma('sp', mu_sb[:], mu_fm, ('a', 1), writes=['mu'])
                for li in range(3):
                    P.op('dve', (lambda li=li: lambda e: e.memset(lds[li][:, 0:1], 0.0))(), writes=[('ld0', li)])
                P.op('dve', lambda e: e.memset(tiny[:], 1e-24), writes=['tiny'])

                def load_mix(row0, nrow, mucol, dst, dkey, li=0, eng='dve'):
                    ld = lds[li]
                    P.dma('sp', ld[0:nrow, 1:2049], projT_d[row0:row0 + nrow, :], ('a', 2 + li), writes=[('ld', li)])
                    P.tt(eng, T1[0:nrow, :], ld[0:nrow, 0:2048], ld[0:nrow, 1:2049], ALU.subtract, reads=[('ld', li), ('ld0', li)], writes=['T1'])
                    if eng == 'pool':
                        P.tt(eng, T1[0:nrow, :], T1[0:nrow, :], mu_sb[0:nrow, mucol:mucol + 1].to_broadcast([nrow, S]), ALU.mult, reads=['T1', 'mu'], writes=['T1'])
                        P.tt(eng, dst, T1[0:nrow, :], ld[0:nrow, 1:2049], ALU.add, reads=['T1', ('ld', li)], writes=[dkey])
                    else:
                        P.stt(eng, dst, T1[0:nrow, :], mu_sb[0:nrow, mucol:mucol + 1], ld[0:nrow, 1:2049], ALU.mult, ALU.add,
                              reads=['T1', ('ld', li), 'mu'], writes=[dkey])

                def mixes(f, eng):
                    load_mix(10240 + f * 128, 128, 32 + f, vb, 'vb', 0, eng)
                    load_mix(8192 + f * 128, 128, 16 + f, T3, 'T3', 1, eng)
                    load_mix(6144 + f * 128, 128, f, T2, 'T2', 2, eng)
                load_mix(12288, 96, 48, T2[0:96, :], 'T2')
                P.act(txw[0:96, :], T2[0:96, :], AF.Tanh, reads=['T2'], writes=['txw'])
                load_mix(12384, 96, 49, xa_m[0:96, :], 'xa_m')
                for j in range(2):
                    load_mix(12480 + j * 128, 128, 50 + j, T2[:, :], 'T2')
                    P.act(sxg[:, j, :], T2[:, :], AF.Sigmoid, reads=['T2'], writes=[('sxg', j)])

                def blocksum(src, skey):
                    for b in range(4):
                        P.mm(bank(b), blockones[:], src[:, b * 512:(b + 1) * 512], reads=[skey, 'blockones'], writes=[('ps', b)])
                PSA = PS[:, 0:2048]
                PSB = PS[:, 2048:4096]
                psk4 = [('ps', b) for b in range(4)]
                psk8 = [('ps', b) for b in range(4, 8)]
                mixes(0, 'dve')
                for f in range(16):
                    col = lambda q: rwp_sb[:, q, f:f + 1]
                    for b in range(4):
                        P.mm(bank(b), w_dec_b[0:96, f * 128:(f + 1) * 128], txw[0:96, b * 512:(b + 1) * 512],
                             reads=['w_dec_b', 'txw'], writes=[('ps', b)])
                    for b in range(4):
                        P.mm(bank(4 + b), w_aaa_b[0:96, f * 128:(f + 1) * 128], xa_m[0:96, b * 512:(b + 1) * 512],
                             reads=['w_aaa_b', 'xa_m'], writes=[('ps', 4 + b)])
                    P.act(LW, PSA, AF.Sigmoid, bias=col(0), reads=psk4 + ['rwp'], writes=['LW'])
                    P.act(T4, PSB, AF.Sigmoid, bias=col(1), reads=psk8 + ['rwp'], writes=['T4'])
                    P.ts('dve', T1, T3, col(2), None, ALU.mult, reads=['T3', 'rwp'], writes=['T1'])
                    P.tt('dve', T5, T1, T1, ALU.mult, reads=['T1'], writes=['T5'])
                    blocksum(T5, 'T5')
                    P.op('dve', lambda e: e.tensor_tensor_scan(out=LG, data0=rmask[:], data1=LW, initial=0.0, op0=ALU.mult, op1=ALU.add),
                         reads=['LW', 'rmask'], writes=['LG'])
                    P.ts('dve', TMPK, T4, -1.0, col(3), ALU.add, ALU.mult, reads=['T4', 'rwp'], writes=['BT', 'KT'])
                    P.stt('dve', T3, TMPK, 1.0, T3, ALU.add, ALU.mult, reads=['BT', 'KT', 'T3'], writes=['T3'])
                    P.ts('dve', T5, PSA, tiny[:, 0:1], None, ALU.max, reads=psk4 + ['tiny'], writes=['T5'])
                    P.act(T5, T5, AF.Ln, reads=['T5'], writes=['T5'])
                    P.act(T5, T5, AF.Exp, scale=-0.5, reads=['T5'], writes=['T5'])
                    lgc = v3(LG)[:, :, 63:64]
                    P.tt('dve', LW, LG, LW, ALU.subtract, reads=['LG', 'LW'], writes=['LW'])
                    P.tt('dve', v3(BON), v3(LG), lgc.to_broadcast([128, 32, 64]), ALU.subtract, reads=['LG'], writes=['BON', 'Pb', 'PTb'])
                    P.act(PA2, LG, AF.Exp, scale=NEG_E, reads=['LG'], writes=['Pa', 'PTa'])
                    P.act(GC[:].rearrange("p (c o) -> p c o", o=1), lgc, AF.Exp, scale=NEG_E, reads=['LG'], writes=['GC'])
                    P.act(LW, LW, AF.Exp, scale=NEG_E, reads=['LW'], writes=['LW'])
                    P.act(BON, BON, AF.Exp, scale=-NEG_E, reads=['BON', 'Pb', 'PTb'], writes=['BON', 'Pb', 'PTb'])
                    P.act(LG, LG, AF.Exp, scale=-NEG_E, reads=['LG'], writes=['LG'])
                    P.tt('dve', AR[:, :, 64:128], v3(T2), v3(PA2), ALU.mult, reads=['T2', 'Pa', 'PTa'], writes=['ARr'])
                    P.stt('dve', T2, T2, col(4), T3, ALU.mult, ALU.mult, reads=['T2', 'T3', 'rwp'], writes=['T2'])
                    for b in range(4):
                        P.mm(bank(4 + b), blockones[:], T2[:, b * 512:(b + 1) * 512], reads=['T2', 'blockones'], writes=[('ps', 4 + b)])
                    P.cp('act', T2, PSB, reads=psk8, writes=['T2'])
                    P.tt('dve', T1, T1, T5, ALU.mult, reads=['T1', 'T5'], writes=['T1'])
                    P.tt('dve', T4, T1, T4, ALU.mult, reads=['T1', 'T4'], writes=['T4'])
                    P.stt('dve', AR[:, :, 0:64], v3(T1), -1.0, v3(LW), ALU.mult, ALU.mult, reads=['T1', 'LW'], writes=['ARa'])
                    P.tt('dve', BBAR, T4, LG, ALU.mult, reads=['T4', 'LG'], writes=['BBAR'])
                    P.tt('dve', KBAR, T3, LG, ALU.mult, reads=['T3', 'LG'], writes=['KBAR'])
                    P.tt('dve', BT, T4, BON, ALU.mult, reads=['T4', 'BON', 'Pb', 'PTb'], writes=['BT'])
                    P.tt('dve', KT, T3, BON, ALU.mult, reads=['T3', 'BON', 'Pb', 'PTb'], writes=['KT'])
                    for qi, (src, dst, skey, dkey) in enumerate(((vb, Vtm, 'vb', 'Vtm'), (BT, BtT, 'BT', 'BtT'), (KT, KtT, 'KT', 'KtT'))):
                        for half in range(2):
                            bk = (qi * 2 + half) % 8
                            bkb = bank(bk).bitcast(BF16)
                            for c16 in range(16):
                                c = half * 16 + c16
                                for hh in range(2):
                                    ps_ = slice(hh * 64, (hh + 1) * 64)
                                    P.tr(bkb[ps_, c16 * 64:(c16 + 1) * 64], src[ps_, c * 64:(c + 1) * 64], ident_b[ps_, ps_],
                                         reads=[skey, 'ident_b'], writes=[('ps', bk)], sig=(c16 == 15 and hh == 1))
                            P.cp('act' if half else 'dve', dst[:, half * 1024:(half + 1) * 1024], bkb[:, 0:1024], reads=[('ps', bk)], writes=[dkey])
                    for (lhs, dst, lkey, dkey) in ((BBAR, AB, 'BBAR', 'AB'), (KBAR, AK, 'KBAR', 'AK')):
                        for c in range(32):
                            bk = c // 4
                            for hh in range(2):
                                ps_ = slice(hh * 64, (hh + 1) * 64)
                                P.mm(bank(bk)[ps_, (c % 4) * 128:(c % 4 + 1) * 128], lhs[ps_, c * 64:(c + 1) * 64], AR[ps_, c, :],
                                     reads=[lkey, 'ARa', 'ARr'], writes=[('ps', bk)], sig=(c % 4 == 3 and hh == 1))
                            if c % 4 == 3:
                                P.tt('dve', dst[:, c - 3:c + 1, :], bank(bk).rearrange("p (c t) -> p c t", t=128),
                                     msk_si4[:].rearrange("p (c t) -> p c t", t=128), ALU.mult, reads=[('ps', bk), 'msk_si4'], writes=[dkey])
                    for c in range(32):
                        bk = c // 8
                        for hh in range(2):
                            ps_ = slice(hh * 64, (hh + 1) * 64)
                            P.mm(bank(bk)[ps_, (c % 8) * 64:(c % 8 + 1) * 64], AR[ps_, c, 0:64], BBAR[ps_, c * 64:(c + 1) * 64],
                                 reads=['BBAR', 'ARa'], writes=[('ps', bk)], sig=(c % 8 == 7 and hh == 1))
                        if c % 8 == 7:
                            P.tt('dve', PTa[:, (c - 7) * 64:(c + 1) * 64], bank(bk), msk_lo8[:], ALU.mult, reads=[('ps', bk), 'msk_lo8'], writes=['PTa'])
                    P.tt('dve', v3(Tm), AB[:, :, 0:64], identx[:].rearrange("p (o t) -> p o t", o=1).to_broadcast([128, 32, 64]), ALU.add,
                         reads=['AB', 'identx'], writes=['Tm'])
                    Pc = None
                    PTc, PTk = PTa, 'PTa'
                    for lvl in range(1, 6):
                        Pn, PTn = (Pb, PTb) if lvl % 2 else (Pa, PTa)
                        Pnk, PTnk = ('Pb', 'PTb') if lvl % 2 else ('Pa', 'PTa')
                        for c in range(32):
                            for hh in range(2):
                                ps_ = slice(hh * 64, (hh + 1) * 64)
                                Pcur = AB[ps_, c, 0:64] if Pc is None else Pc[ps_, c * 64:(c + 1) * 64]
                                PTcur = PTc[ps_, c * 64:(c + 1) * 64]
                                pk = 'AB' if Pc is None else Pck
                                last = (c % 8 == 7 and hh == 1)
                                if lvl < 5:
                                    P.mm(bank(c // 8)[ps_, (c % 8) * 64:(c % 8 + 1) * 64], PTcur, Pcur, reads=[pk, PTk], writes=[('ps', c // 8)], sig=last)
                                P.mm(bank(4 + c // 8)[ps_, (c % 8) * 64:(c % 8 + 1) * 64], Pcur, PTcur, reads=[pk, PTk], writes=[('ps', 4 + c // 8)], sig=last)
                            if c % 8 == 7:
                                g8 = c // 8
                                if lvl < 5:
                                    P.cp('act', Pn[:, g8 * 512:(g8 + 1) * 512], bank(g8), reads=[('ps', g8)], writes=[Pnk])
                                P.cp('dve', PTn[:, g8 * 512:(g8 + 1) * 512], bank(4 + g8), reads=[('ps', 4 + g8)], writes=[PTnk])
                        for c in range(32):
                            for hh in range(2):
                                ps_ = slice(hh * 64, (hh + 1) * 64)
                                P.mm(bank(c // 8)[ps_, (c % 8) * 64:(c % 8 + 1) * 64], PTn[ps_, c * 64:(c + 1) * 64], Tm[ps_, c * 64:(c + 1) * 64],
                                     reads=[PTnk, 'Tm'], writes=[('ps', c // 8)], sig=(c % 8 == 7 and hh == 1))
                            if c % 8 == 7:
                                g8 = c // 8
                                P.tt('dve', Tm[:, g8 * 512:(g8 + 1) * 512], bank(g8), Tm[:, g8 * 512:(g8 + 1) * 512], ALU.add,
                                     reads=[('ps', g8), 'Tm'], writes=['Tm'])
                        Pc, Pck = Pn, Pnk
                        PTc, PTk = PTn, PTnk
                    P.tt('dve', BON, T2, vb, ALU.mult, reads=['T2', 'vb'], writes=['BON', 'Pb', 'PTb'])
                    if f + 1 < 16:
                        mixes(f + 1, 'pool')
                    P.op('dve', lambda e: e.memset(Zf[:], 0.0), writes=['Zf'])
                    P.op('dve', lambda e: e.memset(Zb[:], 0.0), writes=['Zb'])
                    for c in range(32):
                        yb = 3 + (c // 8) % 2
                        for hh in range(2):
                            ps_ = slice(hh * 64, (hh + 1) * 64)
                            P.mm(bank(0)[ps_, 0:64], AR[ps_, c, 0:64], Zb[ps_, :], start=True, stop=False, reads=['ARa', 'Zb'], writes=[('ps', 0)], sig=False)
                            P.mm(bank(0)[ps_, 0:64], AK[ps_, c, 0:64], Vtm[ps_, c * 64:(c + 1) * 64], start=False, stop=True,
                                 reads=['AK', 'Vtm'], writes=[('ps', 0)], sig=(hh == 1))
                        P.cp('act', Xs[:], bank(0)[:, 0:64], reads=[('ps', 0)], writes=['Xs'])
                        for hh in range(2):
                            ps_ = slice(hh * 64, (hh + 1) * 64)
                            P.mm(bank(1)[ps_, 0:64], Tm[ps_, c * 64:(c + 1) * 64], Xs[ps_, :], reads=['Tm', 'Xs'], writes=[('ps', 1)], sig=(hh == 1))
                        P.cp('dve', Us[:], bank(1)[:, 0:64], reads=[('ps', 1)], writes=['Us'])
                        for hh in range(2):
                            ps_ = slice(hh * 64, (hh + 1) * 64)
                            yo = bank(yb)[ps_, (c % 8) * 64:(c % 8 + 1) * 64]
                            P.mm(yo, Zb[ps_, :], AR[ps_, c, 64:128], start=True, stop=False, reads=['Zb', 'ARr'], writes=[('ps', yb)], sig=False)
                            P.mm(yo, Us[ps_, :], AB[ps_, c, 64:128], start=False, stop=False, reads=['Us', 'AB'], writes=[('ps', yb)], sig=False)
                            P.mm(yo, Vtm[ps_, c * 64:(c + 1) * 64], AK[ps_, c, 64:128], start=False, stop=True, reads=['Vtm', 'AK'], writes=[('ps', yb)], sig=(hh == 1))
                        for hh in range(2):
                            ps_ = slice(hh * 64, (hh + 1) * 64)
                            P.mm(bank(2)[ps_, 0:64], BtT[ps_, c * 64:(c + 1) * 64], Us[ps_, :], start=True, stop=False, reads=['BtT', 'Us'], writes=[('ps', 2)], sig=False)
                            P.mm(bank(2)[ps_, 0:64], KtT[ps_, c * 64:(c + 1) * 64], Vtm[ps_, c * 64:(c + 1) * 64], start=False, stop=True,
                                 reads=['KtT', 'Vtm'], writes=[('ps', 2)], sig=(hh == 1))
                        P.stt('dve', Zb[:], Zf[:], GC[:, c:c + 1], bank(2)[:, 0:64], ALU.mult, ALU.add, reads=['Zf', 'GC', ('ps', 2)], writes=['Zb'])
                        P.stt('dve', Zf[:], Zf[:], GC[:, c:c + 1], bank(2)[:, 0:64], ALU.mult, ALU.add, reads=['Zf', 'GC', ('ps', 2)], writes=['Zf'])
                        if c % 8 == 7:
                            g8 = c // 8
                            P.cp('act', Ysb[:, g8 * 512:(g8 + 1) * 512], bank(yb), reads=[('ps', yb)], writes=['LW'])
                    blocksum(Ysb, 'LW')
                    P.stt('dve', T5, PSA, -1.0 / 64.0, Ysb, ALU.mult, ALU.add, reads=psk4 + ['LW'], writes=['T5'])
                    P.tt('dve', T1, T5, T5, ALU.mult, reads=['T5'], writes=['T1'])
                    blocksum(T1, 'T1')
                    P.act(T1, PSA, AF.Ln, scale=1.0 / 64.0, bias=gn_eps_col[:, 0:1], reads=psk4 + ['gn_eps_col'], writes=['T1'])
                    P.act(T1, T1, AF.Exp, scale=-0.5, reads=['T1'], writes=['T1'])
                    P.stt('dve', T5, T5, col(5), T1, ALU.mult, ALU.mult, reads=['T5', 'T1', 'rwp'], writes=['T5'])
                    P.stt('dve', T5, T5, col(6), BON, ALU.add, ALU.add, reads=['T5', 'BON', 'Pb', 'PTb', 'rwp'], writes=['T5'])
                    for b in range(4):
                        for j in range(2):
                            P.mm(bank(4 + b), w_gate_b[:, j, f * 128:(f + 1) * 128], sxg[:, j, b * 512:(b + 1) * 512], start=(j == 0), stop=(j == 1),
                                 reads=['w_gate_b', ('sxg', 0), ('sxg', 1)], writes=[('ps', 4 + b)], sig=(j == 1))
                    P.tt('dve', ostg4, T5, PSB, ALU.mult, reads=['T5'] + psk8, writes=['ostg4'])
                    P.dma('sp', oT_d[2048 + f * 128:2048 + (f + 1) * 128, :], ostg4, ('s', 0), reads=['ostg4'], writes=['oT'])
                P.barrier()
        elif stop_after >= 5:
            with ExitStack() as ph:
                zt = sb("zt", [128, S], BF16, ph)
                P.op('dve', lambda e: e.memset(zt[:], 0.0), writes=['zt'])
                for f in range(16):
                    P.dma('sp', oT_d[2048 + f * 128:2048 + (f + 1) * 128, :], zt[:], ('s', f % 2), reads=['zt'], writes=['oT'])
                P.barrier()

        if stop_after >= 5:
            oTb = big[:, 0:65536].rearrange("p (k t) -> p k t", t=S)
            with ExitStack() as ph:
                wbuf = [sb("wobuf%d" % i, [128, 32, 128], BF16, ph) for i in range(3)]
                xt_sb = [sb("xt_sb%d" % i, [128, 512], F32, ph) for i in range(2)]
                x1s = [sb("x1s%d" % i, [128, 512], F32, ph) for i in range(2)]
                sqb = [sb("sqb%d" % i, [128, 512], BF16, ph) for i in range(2)]
                ssq = sb("ssq", [128, S], F32, ph)
                rstd2 = ssq
                tmp5 = x1s
                oT_dv = oT_d.rearrange("(k p) t -> p k t", p=128)
                for q in range(8):
                    P.dma('sp', oTb[:, q * 4:(q + 1) * 4, :], oT_dv[:, q * 4:(q + 1) * 4, :], ('a', q), writes=[('oTb', q)])
                P.op('dve', lambda e: e.memset(ssq[:], 0.0), writes=['ssq'])
                wv = w_out.rearrange("(k p) c -> p k c", p=128)
                pend = None
                u = 0
                for ct in range(32):
                    slot = ct % 3
                    P.dma('pool', wbuf[slot][:], wv[:, :, ct * 128:(ct + 1) * 128], ('w', slot), writes=[('wb', slot)])
                    for tt in range(4):
                        bk = u % 4
                        sbk = 4 + u % 4
                        us = u % 2
                        P.dma('sp', xt_sb[us][:], xT_d[ct * 128:(ct + 1) * 128, tt * 512:(tt + 1) * 512], ('a', 8 + us), writes=[('xt', us)])
                        for k in range(32):
                            P.mm(bank(bk), wbuf[slot][:, k, :], oTb[:, k, tt * 512:(tt + 1) * 512], start=(k == 0), stop=(k == 31),
                                 reads=[('wb', slot)] + ([('oTb', k // 4)] if ct == 0 else []), writes=[('ps', bk)], sig=(k == 31))
                        if pend is not None:
                            pend()
                        P.stt('dve', x1s[us][:], bank(bk), gt_m[:, ct:ct + 1], xt_sb[us][:], ALU.mult, ALU.add,
                              reads=[('ps', bk), ('xt', us), 'mod'], writes=[('x1s', us)])
                        P.act(sqb[us][:], x1s[us][:], AF.Square, reads=[('x1s', us)], writes=[('sqb', us)])
                        P.dma('sp', x1T_d[ct * 128:(ct + 1) * 128, tt * 512:(tt + 1) * 512], x1s[us][:], ('s', us), reads=[('x1s', us)], writes=['x1T'])

                        def mk(us=us, sbk=sbk, tt=tt):
                            def f():
                                P.mm(bank(sbk), ones_b[:], sqb[us][:], reads=[('sqb', us), 'ones_b'], writes=[('ps', sbk)])
                                P.tt('dve', ssq[:, tt * 512:(tt + 1) * 512], bank(sbk), ssq[:, tt * 512:(tt + 1) * 512], ALU.add,
                                     reads=[('ps', sbk), 'ssq'], writes=['ssq'])
                            return f
                        pend = mk()
                        u += 1
                pend()
                P.barrier()
                h2T = big[:, 0:65536].rearrange("p (k t) -> p k t", t=S)
                P.act(rstd2[:], ssq[:], AF.Sqrt, scale=1.0 / D, bias=eps_col[:, 0:1], reads=['ssq', 'eps_col'], writes=['ssq', 'rstd2'])
                P.op('dve', lambda e: e.reciprocal(out=rstd2[:], in_=rstd2[:]), reads=['rstd2'], writes=['rstd2'])
                xrow = [wbuf[i][:].rearrange("p k c -> p (k c)").bitcast(F32) for i in range(3)]
                for ct in range(32):
                    rs = ct % 3
                    P.dma('sp' if ct % 2 == 0 else 'pool', xrow[rs], x1T_d[ct * 128:(ct + 1) * 128, :],
                          ('a', 8 + rs) if ct % 2 == 0 else ('s', 4 + rs), writes=[('xrow', rs)])
                    P.tt('dve', xrow[rs], xrow[rs], rstd2[:], ALU.mult, reads=[('xrow', rs), 'rstd2'], writes=[('xrow', rs)])
                    P.act(h2T[:, ct, :], xrow[rs], AF.Identity, scale=gam_f[:, ct:ct + 1], bias=sh_f[:, ct:ct + 1],
                          reads=[('xrow', rs), 'gam_f', 'mod'], writes=[('h2T', ct)])
                P.barrier()

        if stop_after >= 6:
            h2T = big[:, 0:65536].rearrange("p (k t) -> p k t", t=S)
            with ExitStack() as ph:
                wu = [sb("wu%d" % i, [128, 32, 128], BF16, ph) for i in range(2)]
                wvb = [sb("wvb%d" % i, [128, 32, 128], BF16, ph) for i in range(2)]
                u_sb = [big[:, 65536 + i * 4104:65536 + i * 4104 + 4100].bitcast(F32) for i in range(2)]
                uc = [sb("uc%d" % i, [128, 1024], F32, ph) for i in range(2)]
                aT = [sb("aT%d" % i, [128, 1024], BF16, ph) for i in range(2)]
                conv_sb = sb("conv_sb", [128, 4, FC], F32, ph)
                P.dma('sp', conv_sb[:], conv_fm, ('a', 0), writes=['conv'])
                for i in range(2):
                    P.op('dve', (lambda i=i: lambda e: e.memset(u_sb[i][:, 0:2], 0.0))(), writes=[('u', i, 0)])
                wv = w_up.rearrange("(k p) c -> p k c", p=128)
                n = 0
                for j in range(FC):
                    slot = j % 2
                    P.dma('pool', wu[slot][:], wv[:, :, j * 128:(j + 1) * 128], ('w', slot), writes=[('wu', slot)])
                    P.dma('pool', wvb[slot][:], wv[:, :, DFF + j * 128:DFF + (j + 1) * 128], ('w', 2 + slot), writes=[('wv', slot)])
                    ub = u_sb[j % 2]
                    for th in range(2):
                        base = (n % 2) * 4
                        ns = n % 2
                        for t2 in range(2):
                            tok = th * 1024 + t2 * 512
                            for k in range(32):
                                P.mm(bank(base + t2), wu[slot][:, k, :], h2T[:, k, tok:tok + 512], start=(k == 0), stop=(k == 31),
                                     reads=[('wu', slot)], writes=[('ps', base + t2)], sig=(k == 31))
                        for t2 in range(2):
                            tok = th * 1024 + t2 * 512
                            for k in range(32):
                                P.mm(bank(base + 2 + t2), wvb[slot][:, k, :], h2T[:, k, tok:tok + 512], start=(k == 0), stop=(k == 31),
                                     reads=[('wv', slot)], writes=[('ps', base + 2 + t2)], sig=(k == 31))
                        o0 = th * 1024
                        P.cp('act', ub[:, 2 + o0:2 + o0 + 1024], PS[:, base * 512:(base + 2) * 512],
                             reads=[('ps', base), ('ps', base + 1)], writes=[('u', j % 2, 1 + th)])
                        rd = [('u', j % 2, 0), ('u', j % 2, 1), ('u', j % 2, 2), 'conv']
                        P.ts('dve', uc[ns][:], ub[:, o0:o0 + 1024], conv_sb[:, 0, j:j + 1], conv_sb[:, 3, j:j + 1], ALU.mult, ALU.add,
                             reads=rd, writes=[('uc', ns)])
                        P.stt('dve', uc[ns][:], ub[:, 1 + o0:1 + o0 + 1024], conv_sb[:, 1, j:j + 1], uc[ns][:], ALU.mult, ALU.add,
                              reads=rd + [('uc', ns)], writes=[('uc', ns)])
                        P.stt('dve', uc[ns][:], ub[:, 2 + o0:2 + o0 + 1024], conv_sb[:, 2, j:j + 1], uc[ns][:], ALU.mult, ALU.add,
                              reads=rd + [('uc', ns)], writes=[('uc', ns)])
                        P.act(uc[ns][:], uc[ns][:], AF.Silu, reads=[('uc', ns)], writes=[('uc', ns)])
                        P.tt('dve', aT[ns][:], uc[ns][:], PS[:, (base + 2) * 512:(base + 4) * 512], ALU.mult,
                             reads=[('uc', ns), ('ps', base + 2), ('ps', base + 3)], writes=[('aT', ns)])
                        P.dma('sp', actT_d[j * 128:(j + 1) * 128, o0:o0 + 1024], aT[ns][:], ('s', ns), reads=[('aT', ns)], writes=['actT'])
                        n += 1
                P.barrier()

        if stop_after >= 7:
            aTv = big[:, 0:44032].rearrange("p (k t) -> p k t", t=512)
            x2T = big[:, 45056:77824].bitcast(F32).rearrange("p (k t) -> p k t", t=512)
            with ExitStack() as ph:
                wd = [sb("wd%d" % i, [128, 22, 128], BF16, ph) for i in range(4)]
                xt_sb = [sb("xt7_%d" % i, [128, 512], F32, ph) for i in range(2)]
                sqb = [sb("sqb7_%d" % i, [128, 512], BF16, ph) for i in range(2)]
                ssq = sb("ssq7", [128, 512], F32, ph)
                rstd3 = sb("rstd3", [128, 512], F32, ph)
                yT = [sb("yT%d" % i, [128, 512], F32, ph) for i in range(2)]
                ostage = [sb("ostage%d" % i, [128, 512], F32, ph) for i in range(4)]
                wv = w_down.rearrange("(k p) c -> p k c", p=128)
                aT_dv = actT_d.rearrange("(k p) t -> p k t", p=128)
                gfin = gvec_sb[:, 2, :]
                kgs = [(0, 22), (22, 22), (44, 21), (65, 21)]
                wn = 0
                osn = 0
                def load_aT(tt):
                    for q, (k0, nk) in enumerate(kgs):
                        P.dma('sp', aTv[:, k0:k0 + nk, :], aT_dv[:, k0:k0 + nk, tt * 512:(tt + 1) * 512], ('a', q), writes=[('aTv', q)])
                load_aT(0)
                for tt in range(4):
                    P.op('dve', lambda e: e.memset(ssq[:], 0.0), writes=['ssq7'])
                    pend = None
                    for ct in range(32):
                        bk = ct % 4
                        sbk = 4 + ct % 4
                        us = ct % 2
                        P.dma('sp', xt_sb[us][:], x1T_d[ct * 128:(ct + 1) * 128, tt * 512:(tt + 1) * 512], ('a', 8 + us), writes=[('xt', us)])
                        for q, (k0, nk) in enumerate(kgs):
                            slot = wn % 4
                            wn += 1
                            P.dma('pool', wd[slot][:, 0:nk, :], wdb_d[ct].rearrange("p (k c) -> p k c", c=128)[:, k0:k0 + nk, :], ('w', slot), writes=[('wd', slot)])
                            for kk in range(nk):
                                P.mm(bank(bk), wd[slot][:, kk, :], aTv[:, k0 + kk, :], start=(q == 0 and kk == 0), stop=(q == 3 and kk == nk - 1),
                                     reads=[('wd', slot), ('aTv', q)], writes=[('ps', bk)], sig=(kk == nk - 1))
                        if pend is not None:
                            pend()
                        P.stt('dve', x2T[:, ct, :], bank(bk), gt_f[:, ct:ct + 1], xt_sb[us][:], ALU.mult, ALU.add,
                              reads=[('ps', bk), ('xt', us), 'mod'], writes=[('x2T', ct)])
                        P.act(sqb[us][:], x2T[:, ct, :], AF.Square, reads=[('x2T', ct)], writes=[('sqb', us)])

                        def mk(us=us, sbk=sbk):
                            def f():
                                P.mm(bank(sbk), ones_b[:], sqb[us][:], reads=[('sqb', us), 'ones_b'], writes=[('ps', sbk)])
                                P.tt('dve', ssq[:], bank(sbk), ssq[:], ALU.add, reads=[('ps', sbk), 'ssq7'], writes=['ssq7'])
                            return f
                        pend = mk()
                    pend()
                    if tt + 1 < 4:
                        load_aT(tt + 1)
                    P.act(rstd3[:], ssq[:], AF.Sqrt, scale=1.0 / D, bias=eps_col[:, 0:1], reads=['ssq7', 'eps_col'], writes=['rstd3'])
                    P.op('dve', lambda e: e.reciprocal(out=rstd3[:], in_=rstd3[:]), reads=['rstd3'], writes=['rstd3'])
                    for cg in range(8):
                        for c4 in range(4):
                            ct = cg * 4 + c4
                            ys = ct % 2
                            P.tt('dve', yT[ys][:], x2T[:, ct, :], rstd3[:], ALU.mult, reads=[('x2T', ct), 'rstd3'], writes=[('yT', ys)])
                            P.act(yT[ys][:], yT[ys][:], AF.Copy, scale=gfin[:, ct:ct + 1], reads=[('yT', ys), 'gvec'], writes=[('yT', ys)])
                            for tb4 in range(4):
                                bkk = (cg % 2) * 4 + tb4
                                P.tr(bank(bkk)[:, c4 * 128:(c4 + 1) * 128], yT[ys][:, tb4 * 128:(tb4 + 1) * 128], ident_f[:],
                                     reads=[('yT', ys), 'ident_f'], writes=[('ps', bkk)], sig=(tb4 == 3))
                        for tb4 in range(4):
                            bkk = (cg % 2) * 4 + tb4
                            so = osn % 4
                            osn += 1
                            P.cp('act' if tb4 % 2 == 0 else 'dve', ostage[so][:], bank(bkk), reads=[('ps', bkk)], writes=[('ostage', so)])
                            t0 = tt * 512 + tb4 * 128
                            P.dma('sp', out[t0:t0 + 128, cg * 512:(cg + 1) * 512], ostage[so][:], ('s', so), reads=[('ostage', so)])
                P.barrier()


        if dbg:
            P.barrier()
            D_('sp', dbg_out["d_oT"], oT_d, None, ('s', 0))
            D_('sp', dbg_out["d_x1T"], x1T_d, None, ('s', 1))
        P.barrier()
        P.emit()
    return nc


def prep_inputs(inputs):
    g = lambda n: np.asarray(inputs[n], np.float32)
    shared = {}
    shared["w_ada"] = np.ascontiguousarray(g("w_ada")[0].reshape(32, 128, 192, 128).transpose(2, 1, 0, 3)).reshape(192, 128, D)
    shared["b_ada_fm"] = fm(g("b_ada")[0])
    shared["gvec"] = np.ascontiguousarray(np.stack([fm(g("g_norm_mix")[0]), fm(g("g_norm_ffn")[0]), fm(g("g_norm_final"))], axis=1))
    shared["w_in"] = np.ascontiguousarray(g("w_in")[0])
    mu = g("mu_shift")[0]
    mu_fm = np.zeros((128, 52), np.float32)
    mu_fm[:, 0:48] = fm(mu[0:6144])
    mu_fm[0:96, 48] = mu[6144:6240]
    mu_fm[0:96, 49] = mu[6240:6336]
    mu_fm[:, 50] = mu[6336:6464]
    mu_fm[:, 51] = mu[6464:6592]
    shared["mu_fm"] = mu_fm
    shared["rwp"] = np.ascontiguousarray(np.stack([fm(g(n)[0]) for n in ("w0", "a0", "k_k", "k_a", "r_k", "ln_x_w", "ln_x_b")], axis=1))
    shared["w_dec"] = np.ascontiguousarray(g("w_decay_up")[0])
    shared["w_aaa"] = np.ascontiguousarray(g("w_aaa_up")[0])
    shared["w_gate"] = np.ascontiguousarray(g("w_gate_up")[0])
    shared["gsb_fm"] = np.ascontiguousarray(g("g_sb_out")[0].T)
    shared["w_out"] = np.ascontiguousarray(g("w_out")[0])
    shared["w_up"] = np.ascontiguousarray(g("w_up")[0])
    cw = g("conv_w")[0]
    shared["conv_fm"] = np.ascontiguousarray(np.stack([fm(cw[0]), fm(cw[1]), fm(cw[2]), fm(g("conv_b")[0])], axis=1))
    shared["w_down"] = np.ascontiguousarray(g("w_down")[0])
    for k, v in host_consts().items():
        shared["c_" + k] = v
    x = g("x")
    c = g("c")
    per = []
    for b in range(x.shape[0]):
        m = dict(shared)
        m["x"] = np.ascontiguousarray(x[b])
        m["c_fm"] = fm(c[b])
        per.append(m)
    return per


_NC = None


def kernel(**inputs):
    global _NC
    per = prep_inputs(inputs)
    if _NC is None:
        _NC = build()
    res = run_bass_kernel_spmd(_NC, per, core_ids=list(range(8)))
    return np.stack([np.asarray(r["out"], np.float32) for r in res.results], axis=0)
```
